# Optimizing a Trainium2 kernel written in Bass

```python
import jax, jax.numpy as jnp
from jax import lax
import numpy as np

D_MODEL = 2048
BATCH = 16
SEQ = 2048
DEPTH = 4
DEC_BATCH = 32
DEC_SEQ = 64
PAST_LEN = 2048

CHUNK = 64
N_A_LAYERS = DEPTH // 2
N_B_LAYERS = DEPTH - N_A_LAYERS
D_RNN = D_MODEL
RG_BLOCKS = 8
RG_BLOCK_W = D_RNN // RG_BLOCKS
CONV_W = 4
RG_C = 8.0
N_HEADS = 16
HEAD_DIM = D_MODEL // N_HEADS
LEFT_CHUNKS = 8
LEFT = LEFT_CHUNKS * CHUNK
BAND = LEFT + CHUNK
REL_CLIP = 128
D_FF = ((8 * D_MODEL // 3 + 255) // 256) * 256
EPS = 1e-6

kernel_name = "hawk_yoco_chunk_band_stream_step"


def _rms_norm(x, g):
    xf = x.astype(jnp.float32)
    y = xf * lax.rsqrt(jnp.mean(xf * xf, axis=-1, keepdims=True) + EPS)
    return (y * g.astype(jnp.float32)).astype(x.dtype)


def _modulate(x, shift, scale):
    return x * (1 + scale[:, None, :]) + shift[:, None, :]


def _causal_conv(x, prev, w, b):
    if prev is None:
        prev = jnp.zeros((x.shape[0], CONV_W - 1, x.shape[2]), x.dtype)
    xp = jnp.concatenate([prev.astype(x.dtype), x], axis=1)
    S = x.shape[1]
    y = b + xp[:, 0:S] * w[0]
    for k in range(1, CONV_W):
        y = y + xp[:, k:k + S] * w[k]
    return y, xp[:, -(CONV_W - 1):]


def _lru_combine(left, right):
    a1, b1 = left
    a2, b2 = right
    return a1 * a2, a2 * b1 + b2


def _rg_lru(x, h0, w_a, b_a, w_i, b_i, lam, pos0):
    B, S, C = x.shape
    xb = x.reshape(B, S, RG_BLOCKS, RG_BLOCK_W)
    r = jax.nn.sigmoid(jnp.einsum('bsni,nij->bsnj', xb, w_a).reshape(B, S, C) + b_a)
    i = jax.nn.sigmoid(jnp.einsum('bsni,nij->bsnj', xb, w_i).reshape(B, S, C) + b_i)
    log_a = -RG_C * r.astype(jnp.float32) * jax.nn.softplus(-lam.astype(jnp.float32))
    a = jnp.exp(log_a)
    mult = jnp.sqrt(-jnp.expm1(2.0 * log_a))
    pos = pos0 + jnp.arange(S)
    mult = jnp.where((pos == 0)[None, :, None], 1.0, mult)
    u = mult * (i * x).astype(jnp.float32)
    a_cum, h = lax.associative_scan(_lru_combine, (a, u), axis=1)
    if h0 is not None:
        h = h + a_cum * h0.astype(jnp.float32)[:, None, :]
    return h.astype(x.dtype), h[:, -1]


def _recurrent_block(xn, conv_prev, h0, p, a, pos0):
    gate = jax.nn.gelu(xn @ p['rg_w_gate'][a])
    xb = xn @ p['rg_w_in'][a]
    xc, conv_new = _causal_conv(xb, conv_prev, p['rg_conv_w'][a], p['rg_conv_b'][a])
    h, h_last = _rg_lru(xc, h0, p['rg_w_a'][a], p['rg_b_a'][a], p['rg_w_i'][a], p['rg_b_i'][a],
                        p['rg_lambda'][a], pos0)
    return (h * gate) @ p['rg_w_out'][a], conv_new, h_last


def _band_attention(q, k_new, v_new, k_past, v_past, rel_table):
    B, S, H, Dh = q.shape
    n_chunks = -(-S // CHUNK)
    s_pad = n_chunks * CHUNK
    if k_past is not None:
        k_past = k_past[:, -LEFT:]
        v_past = v_past[:, -LEFT:]
    n_past = 0 if k_past is None else k_past.shape[1]

    def assemble(new, past):
        parts = [jnp.zeros((B, LEFT - n_past, H, Dh), new.dtype)]
        if past is not None:
            parts.append(past.astype(new.dtype))
        parts += [new, jnp.zeros((B, s_pad - S, H, Dh), new.dtype)]
        return jnp.concatenate(parts, axis=1)

    k_full = assemble(k_new, k_past)
    v_full = assemble(v_new, v_past)
    rows = jnp.arange(LEFT + s_pad)
    valid = (rows >= LEFT - n_past) & (rows < LEFT + S)
    q_blocks = jnp.pad(q, ((0, 0), (0, s_pad - S), (0, 0), (0, 0)))
    q_blocks = q_blocks.reshape(B, n_chunks, CHUNK, H, Dh).transpose(1, 0, 2, 3, 4)
    qi = jnp.arange(CHUNK)[:, None]
    kj = jnp.arange(BAND)[None, :]
    rel_idx = jnp.clip(LEFT + qi - kj, -REL_CLIP, REL_CLIP) + REL_CLIP
    bias = rel_table[rel_idx].astype(jnp.float32).transpose(2, 0, 1)
    scale = HEAD_DIM ** -0.5

    def one_chunk(args):
        qb, c = args
        start = c * CHUNK
        kb = lax.dynamic_slice_in_dim(k_full, start, BAND, axis=1)
        vb = lax.dynamic_slice_in_dim(v_full, start, BAND, axis=1)
        vmask = lax.dynamic_slice_in_dim(valid, start, BAND, axis=0)
        s = jnp.einsum('bqhd,bkhd->bhqk', qb, kb).astype(jnp.float32) * scale + bias
        s = jnp.where(vmask, s, -1e30)
        pr = jax.nn.softmax(s, axis=-1).astype(vb.dtype)
        return jnp.einsum('bhqk,bkhd->bqhd', pr, vb)

    out = lax.map(one_chunk, (q_blocks, jnp.arange(n_chunks)))
    return out.transpose(1, 0, 2, 3, 4).reshape(B, s_pad, H, Dh)[:, :S]


def _trunk(x, c, pos0, conv_state, rnn_state, k_cache, v_cache, p):
    B, S, _ = x.shape
    cs = jax.nn.silu(c)
    conv_out, rnn_out = [], []
    k_new = None
    v_new = None
    for layer in range(DEPTH):
        mod = cs @ p['ada_w'][layer] + p['ada_b'][layer]
        sh1, sc1, g1, sh2, sc2, g2 = jnp.split(mod, 6, axis=-1)
        if layer == N_A_LAYERS:
            kvn = _rms_norm(x, p['g_kv'])
            k_new = (kvn @ p['w_k']).reshape(B, S, N_HEADS, HEAD_DIM)
            v_new = (kvn @ p['w_v']).reshape(B, S, N_HEADS, HEAD_DIM)
        hn = _modulate(_rms_norm(x, p['g_mix'][layer]), sh1, sc1)
        if layer < N_A_LAYERS:
            cp = None if conv_state is None else conv_state[layer]
            h0 = None if rnn_state is None else rnn_state[layer]
            out, cst, hst = _recurrent_block(hn, cp, h0, p, layer, pos0)
            conv_out.append(cst)
            rnn_out.append(hst)
        else:
            bl = layer - N_A_LAYERS
            q = (hn @ p['w_q'][bl]).reshape(B, S, N_HEADS, HEAD_DIM)
            o = _band_attention(q, k_new, v_new, k_cache, v_cache, p['rel_bias'][bl])
            out = o.reshape(B, S, N_HEADS * HEAD_DIM) @ p['w_o'][bl]
        x = x + g1[:, None, :] * out
        hf = _modulate(_rms_norm(x, p['g_ffn'][layer]), sh2, sc2)
        ff = (jax.nn.silu(hf @ p['ffn_w1'][layer]) * (hf @ p['ffn_w3'][layer])) @ p['ffn_w2'][layer]
        x = x + g2[:, None, :] * ff
    y = _rms_norm(x, p['g_final'])
    return y, jnp.stack(conv_out), jnp.stack(rnn_out), k_new, v_new


def setup_inputs(seed: int = 0) -> dict:
    key = jax.random.key(seed)
    ks = iter(jax.random.split(key, 40))
    f32 = jnp.float32

    def nrm(shape, scale):
        return jax.random.normal(next(ks), shape, f32) * scale

    D = D_MODEL
    HHD = N_HEADS * HEAD_DIM
    l_cache = min(LEFT, PAST_LEN)
    u = jax.random.uniform(next(ks), (N_A_LAYERS, D_RNN), f32, 0.9, 0.999)
    a0 = u ** (1.0 / RG_C)
    rg_lambda = jnp.log(a0) - jnp.log1p(-a0)
    return {
        'x_prompt': nrm((BATCH, SEQ, D), 1.0),
        'x_sample': nrm((DEC_BATCH, DEC_SEQ, D), 1.0),
        'c_prompt': nrm((BATCH, D), 1.0),
        'c_sample': nrm((DEC_BATCH, D), 1.0),
        'state_conv': nrm((N_A_LAYERS, DEC_BATCH, CONV_W - 1, D_RNN), 1.0),
        'state_rnn': nrm((N_A_LAYERS, DEC_BATCH, D_RNN), 0.5),
        'cache_k': nrm((DEC_BATCH, l_cache, N_HEADS, HEAD_DIM), 1.0),
        'cache_v': nrm((DEC_BATCH, l_cache, N_HEADS, HEAD_DIM), 1.0),
        'ada_w': nrm((DEPTH, D, 6 * D), D ** -0.5),
        'ada_b': nrm((DEPTH, 6 * D), 0.02),
        'g_mix': 1.0 + nrm((DEPTH, D), 0.02),
        'g_ffn': 1.0 + nrm((DEPTH, D), 0.02),
        'rg_w_in': nrm((N_A_LAYERS, D, D_RNN), D ** -0.5),
        'rg_w_gate': nrm((N_A_LAYERS, D, D_RNN), D ** -0.5),
        'rg_conv_w': nrm((N_A_LAYERS, CONV_W, D_RNN), CONV_W ** -0.5),
        'rg_conv_b': nrm((N_A_LAYERS, D_RNN), 0.02),
        'rg_w_a': nrm((N_A_LAYERS, RG_BLOCKS, RG_BLOCK_W, RG_BLOCK_W), RG_BLOCK_W ** -0.5),
        'rg_b_a': nrm((N_A_LAYERS, D_RNN), 0.02),
        'rg_w_i': nrm((N_A_LAYERS, RG_BLOCKS, RG_BLOCK_W, RG_BLOCK_W), RG_BLOCK_W ** -0.5),
        'rg_b_i': nrm((N_A_LAYERS, D_RNN), 0.02),
        'rg_lambda': rg_lambda,
        'rg_w_out': nrm((N_A_LAYERS, D_RNN, D), D_RNN ** -0.5),
        'g_kv': 1.0 + nrm((D,), 0.02),
        'w_k': nrm((D, HHD), D ** -0.5),
        'w_v': nrm((D, HHD), D ** -0.5),
        'w_q': nrm((N_B_LAYERS, D, HHD), D ** -0.5),
        'w_o': nrm((N_B_LAYERS, HHD, D), HHD ** -0.5),
        'rel_bias': nrm((N_B_LAYERS, 2 * REL_CLIP + 1, N_HEADS), 0.5),
        'ffn_w1': nrm((DEPTH, D, D_FF), D ** -0.5),
        'ffn_w3': nrm((DEPTH, D, D_FF), D ** -0.5),
        'ffn_w2': nrm((DEPTH, D_FF, D), D_FF ** -0.5),
        'g_final': 1.0 + nrm((D,), 0.02),
    }


def reference(x_prompt, x_sample, c_prompt, c_sample, state_conv, state_rnn, cache_k, cache_v,
              ada_w, ada_b, g_mix, g_ffn, rg_w_in, rg_w_gate, rg_conv_w, rg_conv_b, rg_w_a, rg_b_a,
              rg_w_i, rg_b_i, rg_lambda, rg_w_out, g_kv, w_k, w_v, w_q, w_o, rel_bias,
              ffn_w1, ffn_w3, ffn_w2, g_final):
    p = {
        'ada_w': ada_w, 'ada_b': ada_b, 'g_mix': g_mix, 'g_ffn': g_ffn,
        'rg_w_in': rg_w_in, 'rg_w_gate': rg_w_gate, 'rg_conv_w': rg_conv_w, 'rg_conv_b': rg_conv_b,
        'rg_w_a': rg_w_a, 'rg_b_a': rg_b_a, 'rg_w_i': rg_w_i, 'rg_b_i': rg_b_i,
        'rg_lambda': rg_lambda, 'rg_w_out': rg_w_out, 'g_kv': g_kv, 'w_k': w_k, 'w_v': w_v,
        'w_q': w_q, 'w_o': w_o, 'rel_bias': rel_bias,
        'ffn_w1': ffn_w1, 'ffn_w3': ffn_w3, 'ffn_w2': ffn_w2, 'g_final': g_final,
    }
    y_prompt, conv_prompt, rnn_prompt, k_p, v_p = _trunk(x_prompt, c_prompt, 0, None, None, None, None, p)
    keep = min(LEFT, x_prompt.shape[1])
    k_prompt = k_p[:, -keep:]
    v_prompt = v_p[:, -keep:]
    y_sample, conv_sample, rnn_sample, k_sample, v_sample = _trunk(
        x_sample, c_sample, PAST_LEN, state_conv, state_rnn, cache_k, cache_v, p)
    return (y_prompt, y_sample, conv_prompt, rnn_prompt, k_prompt, v_prompt,
            conv_sample, rnn_sample, k_sample, v_sample)
```

```python
import contextlib
import numpy as np
import concourse.bass as bass
import concourse.mybir as mybir
from concourse.bass_utils import run_bass_kernel_spmd

F32 = mybir.dt.float32
BF16 = mybir.dt.bfloat16
AF = mybir.ActivationFunctionType
ALU = mybir.AluOpType

D = 2048
NCH = 16
FF = 5632
NFC = 44
NGRP = 4
GSZ = 11
DEPTH = 4
EPS = 1e-6
NW = 5
NS = 16
QSCALE = 128 ** -0.5


class Eng:
    def __init__(self, name, sem, inc, kind):
        self.name, self.sem, self.inc, self.kind = name, sem, inc, kind
        self.cnt = 0
        self.seen = {}
        self.prog = []


class Res:
    __slots__ = ("name", "w", "r", "excl")

    def __init__(self, name, excl=False):
        self.name, self.w, self.r, self.excl = name, None, {}, excl


class Sched:
    def __init__(self, nc, stack):
        self.nc = nc
        self.stack = stack
        self.engs = {}
        for name in ("pe", "act", "dve", "pool", "sp"):
            sem = stack.enter_context(nc.semaphore("s_" + name))
            self.engs[name] = Eng(name, sem, 1, name)
        self.nchan = 0
        self.chans = []

    def chan(self):
        self.nchan += 1
        sem = self.stack.enter_context(self.nc.semaphore("c_%d" % self.nchan))
        e = Eng("ch%d" % self.nchan, sem, 16, "chan")
        self.chans.append(e)
        return e

    def op(self, eng, fn, reads=(), writes=(), chan=None):
        eng = self.engs[eng]
        comp = chan or eng
        deps = {}
        for r in reads:
            if r.w is not None:
                e, c = r.w
                if c > deps.get(e, 0):
                    deps[e] = c
            if r.excl:
                for e, c in r.r.items():
                    if e is not comp and c > deps.get(e, 0):
                        deps[e] = c
        for w in writes:
            if w.w is not None:
                e, c = w.w
                if c > deps.get(e, 0):
                    deps[e] = c
            for e, c in w.r.items():
                if (e is not comp) and c > deps.get(e, 0):
                    deps[e] = c
        for e, c in deps.items():
            if e is eng and eng.kind == "pe":
                continue
            if eng.seen.get(e, 0) >= c:
                continue
            eng.seen[e] = c
            eng.prog.append(("w", e.sem, c))
        comp.cnt += comp.inc
        eng.prog.append(("i", fn, comp.sem, comp.inc))
        for w in writes:
            w.w = (comp, comp.cnt)
            w.r = {}
        for r in reads:
            if r not in writes:
                r.r[comp] = comp.cnt

    def barrier(self):
        alle = list(self.engs.values()) + self.chans
        for eng in self.engs.values():
            for e in alle:
                if e is eng or e.cnt == 0:
                    continue
                if eng.seen.get(e, 0) >= e.cnt:
                    continue
                eng.seen[e] = e.cnt
                eng.prog.append(("w", e.sem, e.cnt))

    def final_wait(self, eng):
        eng = self.engs[eng]
        for e in self.chans:
            if e.cnt > 0:
                eng.prog.append(("w", e.sem, e.cnt))

    def emit(self, block):
        def run(prog):
            def f(E):
                for it in prog:
                    if it[0] == "w":
                        E.wait_ge(it[1], it[2])
                    else:
                        it[1](E).then_inc(it[2], it[3])
            return f
        block.tensor(run(self.engs["pe"].prog))
        block.scalar(run(self.engs["act"].prog))
        block.vector(run(self.engs["dve"].prog))
        block.gpsimd(run(self.engs["pool"].prog))
        block.sync(run(self.engs["sp"].prog))


VEC_SPEC = [("g_mix", 4 * 16), ("g_ffn", 4 * 16), ("conv_w", 2 * 4 * 16), ("conv_b", 2 * 16),
            ("b_a", 2 * 16), ("b_i", 2 * 16), ("lam", 2 * 16), ("g_kv", 16), ("g_final", 16),
            ("ada_b", 4 * 6 * 16)]
VEC_OFF = {}
_o = 0
for _n, _s in VEC_SPEC:
    VEC_OFF[_n] = _o
    _o += _s
NV = _o


def build_nc():
    nc = bass.Bass("TRN2", target_bir_lowering=False)

    def din(name, shape, dt=F32):
        return nc.dram_tensor(name, list(shape), dt, kind="ExternalInput").ap()

    def dout(name, shape):
        return nc.dram_tensor(name, list(shape), F32, kind="ExternalOutput").ap()

    xT_p = din("xT_p", [8, 128, 8192])
    xT_s = din("xT_s", [128, 16, 256])
    cT_d = din("cT", [128, 96])
    vecs_d = din("vecs", [128, NV])
    conv_in_d = din("conv_in", [128, 384])
    rnn_in_d = din("rnn_in", [128, 128])
    kTc_d = din("kTc", [4, 128, 8192])
    vc_d = din("vc", [4, 128, 8192])
    dtab_d = din("dtab", [2, 128, 4096])
    cfar_d = din("cfar", [128, 32])
    w_ada = din("w_ada", [4 * 96, 128, 2048])
    w_in = din("w_in", [32, 128, 2048])
    w_gate = din("w_gate", [32, 128, 2048])
    w_ai = din("w_ai", [16, 128, 1024])
    w_out = din("w_out", [32, 128, 2048])
    w_k = din("w_k", [16, 128, 2048])
    w_v = din("w_v", [16, 128, 2048])
    w_q = din("w_q", [32, 128, 2048])
    w_o = din("w_o", [32, 128, 2048])
    w_1 = din("w_1", [4 * 44, 128, 2048])
    w_3 = din("w_3", [4 * 44, 128, 2048])
    w_2 = din("w_2", [4 * 4 * 16, 128, 1408])

    yT_p = dout("yT_p", [8, 128, 16, 512])
    yT_s = dout("yT_s", [128, 16, 256])
    conv_p = dout("conv_p", [2, 128, 96])
    rnn_p = dout("rnn_p", [2, 128, 32])
    conv_s = dout("conv_s", [128, 384])
    rnn_s = dout("rnn_s", [128, 128])
    kT_p = dout("kT_p", [2, 128, 16, 512])
    v_p = dout("v_p", [2, 512, 2048])
    kT_s = dout("kT_s", [128, 16, 256])
    v_s = dout("v_s", [4, 64, 2048])

    with contextlib.ExitStack() as st:
        S = Sched(nc, st)

        def sb(name, shape, dt):
            return st.enter_context(nc.sbuf_tensor("sb_" + name, list(shape), dt))

        x = sb("x", [128, 16, 512], F32)
        hn = sb("hn", [128, 16, 512], BF16)
        mix = sb("mix", [128, 16, 512], BF16)
        bandK = [sb("bK%d" % i, [128, 16, 512], BF16) for i in range(2)]
        bandV = [sb("bV%d" % i, [128, 4, 2048], BF16) for i in range(2)]
        wsl = [sb("wsl%d" % i, [128, 2048], BF16) for i in range(NW)]
        scr = [sb("scr%d" % i, [128, 520], F32) for i in range(NS)]
        expD = sb("expD", [128, 16, 2, 128], BF16)
        modt = sb("modt", [128, 4, 6, 16, 6], F32)
        vecs = sb("vecs", [128, NV], F32)
        kc8 = sb("kc8", [128, 2, 2, 16], F32)
        hb = sb("hb", [128, 2, 32], F32)
        cfar = sb("cfar", [128, 32], F32)
        cTf = sb("cTf", [128, 96], F32)
        cTb = sb("cTb", [128, 16, 6], BF16)
        cst_p = sb("cst_p", [128, 2, 16, 1, 3], F32)
        hst_p = sb("hst_p", [128, 2, 16, 1], F32)
        cst_s = sb("cst_s", [128, 2, 16, 4, 3], F32)
        hst_s = sb("hst_s", [128, 2, 16, 4], F32)
        conv_in = sb("conv_in", [128, 2, 16, 4, 3], F32)
        rnn_in = sb("rnn_in", [128, 2, 16, 4], F32)
        ones_b = sb("ones_b", [128, 128], BF16)
        onesm_b = sb("onesm_b", [128, 128], BF16)
        sptmp = sb("sptmp", [128, 6, 32], F32)
        ps = [st.enter_context(nc.psum_tensor("ps%d" % i, [128, 512], F32)) for i in range(8)]

        Rx = [Res("x%d" % c) for c in range(16)]
        Rhn = [Res("hn%d" % c) for c in range(16)]
        Rmix = [Res("mix%d" % c) for c in range(16)]
        RbK = [[Res("bK%d_%d" % (i, h)) for h in range(16)] for i in range(2)]
        RbV = [[Res("bV%d_%d" % (i, b)) for b in range(4)] for i in range(2)]
        Rw = [Res("w%d" % i) for i in range(NW)]
        Cw = [S.chan() for _ in range(NW)]
        Rs = [Res("scr%d" % i) for i in range(NS)]
        P = [Res("ps%d" % i, excl=True) for i in range(8)]
        RexpD = Res("expD")
        Rmod = Res("mod")
        Rvec = Res("vecs")
        Rkc8 = Res("kc8")
        Rcfar = Res("cfar")
        RcT = Res("cT")
        Rcstp, Rhstp, Rcsts, Rhsts = Res("cstp"), Res("hstp"), Res("csts"), Res("hsts")
        Rcin, Rrin = Res("cin"), Res("rin")
        Rones = Res("ones")
        Rsp = Res("sptmp")
        c_in = S.chan()
        c_x = S.chan()
        c_out = [S.chan() for _ in range(4)]
        c_bk = [S.chan() for _ in range(2)]
        c_bv = [S.chan() for _ in range(2)]

        role_rr = {}

        def bank(role):
            base = {"A": 0, "B": 2, "C": 4, "D": 6}[role]
            k = role_rr.get(role, 0)
            role_rr[role] = k ^ 1
            return base + k

        w_rr = [0]

        def wload(dram_ap, n):
            i = w_rr[0]
            w_rr[0] = (i + 1) % NW
            S.op("pool", lambda E: E.dma_start(out=wsl[i][:, 0:n], in_=dram_ap), writes=[Rw[i]], chan=Cw[i])
            return wsl[i], Rw[i]

        def mmg(out_ap, pairs, reads, pres):
            pairs = list(pairs)

            def fn(E):
                n = len(pairs)
                ins = None
                for k, (l, r) in enumerate(pairs):
                    ins = E.matmul(out_ap, l, r, start=(k == 0), stop=(k == n - 1))
                return ins
            S.op("pe", fn, reads=reads, writes=[pres])

        def mm_kmajor(items, src, Rsrc, nk):
            for k in range(nk):
                def fn(E, k=k):
                    ins = None
                    for (oap, wv, wres, pres) in items:
                        ins = E.matmul(oap, wv[:, k, :], src(k), start=(k == 0), stop=(k == nk - 1))
                    return ins
                S.op("pe", fn, reads=[Rsrc[k]] + [it[2] for it in items], writes=[it[3] for it in items])

        def V_(name, idx):
            o = VEC_OFF[name] + idx
            return vecs[:, o:o + 1]

        S.op("sp", lambda E: E.dma_start(out=vecs[:], in_=vecs_d), writes=[Rvec], chan=c_in)
        S.op("sp", lambda E: E.dma_start(out=cTf[:], in_=cT_d), writes=[RcT], chan=c_in)
        S.op("sp", lambda E: E.dma_start(out=cfar[:], in_=cfar_d), writes=[Rcfar], chan=c_in)
        S.op("sp", lambda E: E.dma_start(out=conv_in[:].rearrange("p a b c d -> p (a b c d)"), in_=conv_in_d), writes=[Rcin], chan=c_in)
        S.op("sp", lambda E: E.dma_start(out=rnn_in[:].rearrange("p a b c -> p (a b c)"), in_=rnn_in_d), writes=[Rrin], chan=c_in)
        for R_ in (Rvec, RcT, Rcfar, Rcin, Rrin):
            R_.w = (c_in, c_in.cnt)
        S.op("dve", lambda E: E.memset(ones_b[:], 1.0), writes=[Rones])
        S.op("dve", lambda E: E.memset(onesm_b[:], 1.0 / D), writes=[Rones])
        S.op("act", lambda E: E.activation(cTb[:].rearrange("p a b -> p (a b)"), cTf[:], AF.Silu), reads=[RcT], writes=[RcT])

        lamv = vecs[:, VEC_OFF["lam"]:VEC_OFF["lam"] + 32]
        t_abs, t_e, t_z, t_w, t_p, t_m = (sptmp[:, k, :] for k in range(6))
        S.op("act", lambda E: E.activation(t_abs, lamv, AF.Abs), reads=[Rvec], writes=[Rsp])
        S.op("act", lambda E: E.activation(t_e, t_abs, AF.Exp, scale=-1.0), reads=[Rsp], writes=[Rsp])
        S.op("dve", lambda E: E.tensor_scalar(t_w, t_e, 2.0, None, ALU.add), reads=[Rsp], writes=[Rsp])
        S.op("dve", lambda E: E.reciprocal(t_w, t_w), reads=[Rsp], writes=[Rsp])
        S.op("dve", lambda E: E.tensor_tensor(t_z, t_e, t_w, ALU.mult), reads=[Rsp], writes=[Rsp])
        S.op("dve", lambda E: E.tensor_tensor(t_w, t_z, t_z, ALU.mult), reads=[Rsp], writes=[Rsp])
        S.op("dve", lambda E: E.memset(t_p, 1.0 / 17), writes=[Rsp])
        for kk in (15, 13, 11, 9, 7, 5, 3, 1):
            S.op("dve", lambda E: E.tensor_tensor(t_p, t_p, t_w, ALU.mult), reads=[Rsp], writes=[Rsp])
            S.op("dve", lambda E, kk=kk: E.tensor_scalar(t_p, t_p, 1.0 / kk, None, ALU.add), reads=[Rsp], writes=[Rsp])
        S.op("dve", lambda E: E.tensor_tensor(t_p, t_p, t_z, ALU.mult), reads=[Rsp], writes=[Rsp])
        S.op("dve", lambda E: E.tensor_scalar(t_m, lamv, -1.0, 0.0, ALU.mult, ALU.max), reads=[Rvec, Rsp], writes=[Rsp])
        S.op("dve", lambda E: E.scalar_tensor_tensor(t_p, t_p, 2.0, t_m, ALU.mult, ALU.add), reads=[Rsp], writes=[Rsp])
        S.op("dve", lambda E: E.tensor_scalar(kc8[:, 0].rearrange("p a b -> p (a b)"), t_p, -4.0, None, ALU.mult), reads=[Rsp], writes=[Rkc8])
        S.op("dve", lambda E: E.tensor_scalar(kc8[:, 1].rearrange("p a b -> p (a b)"), t_p, -8.0, None, ALU.mult), reads=[Rsp], writes=[Rkc8])
        S.op("dve", lambda E: E.tensor_scalar(hb[:, 0, :], vecs[:, VEC_OFF["b_a"]:VEC_OFF["b_a"] + 32], 0.5, None, ALU.mult), reads=[Rvec], writes=[Rkc8])
        S.op("dve", lambda E: E.tensor_scalar(hb[:, 1, :], vecs[:, VEC_OFF["b_i"]:VEC_OFF["b_i"] + 32], 0.5, None, ALU.mult), reads=[Rvec], writes=[Rkc8])

        for L in range(4):
            for j in range(6):
                b = bank("A")
                for n in range(16):
                    wt, wr = wload(w_ada[L * 96 + j * 16 + n], 2048)
                    wv = wt[:].rearrange("p (k n) -> p k n", k=16)
                    mmg(ps[b][:, n * 6:(n + 1) * 6], [(wv[:, k, :], cTb[:, k, :]) for k in range(16)], [wr, RcT], P[b])
                ob = VEC_OFF["ada_b"] + (L * 6 + j) * 16
                for n in range(16):
                    S.op("dve", lambda E, b=b, L=L, j=j, n=n, ob=ob: E.tensor_scalar(
                        modt[:, L, j, n, :], ps[b][:, n * 6:(n + 1) * 6], vecs[:, ob + n:ob + n + 1], None, ALU.add),
                        reads=[P[b], Rvec], writes=[Rmod])
        for L in range(4):
            for (j, gname) in ((1, "g_mix"), (4, "g_ffn")):
                og = VEC_OFF[gname] + L * 16
                for n in range(16):
                    S.op("dve", lambda E, L=L, j=j, n=n, og=og: E.tensor_scalar(
                        modt[:, L, j, n, :], modt[:, L, j, n, :], 1.0, vecs[:, og + n:og + n + 1], ALU.add, ALU.mult),
                        reads=[Rmod, Rvec], writes=[Rmod])

        def M_(L, j, c, s):
            return modt[:, L, j, c, s:s + 1]

        class Tile:
            pass

        def norm_stats(tc, xs, Rxs):
            T = tc.T
            b = getattr(tc, "stats_bank", None)
            tc.stats_bank = None
            if b is None:
                b = bank("D")
                for c in range(16):
                    si = 8 + (c % 2)
                    sqv = scr[si][:].bitcast(BF16)[:, 0:T]
                    if c % 2 == 0:
                        S.op("act", lambda E, sqv=sqv, c=c: E.activation(sqv, xs(c), AF.Square), reads=[Rxs[c]], writes=[Rs[si]])
                    else:
                        S.op("dve", lambda E, sqv=sqv, c=c: E.tensor_tensor(sqv, xs(c), xs(c), ALU.mult), reads=[Rxs[c]], writes=[Rs[si]])

                    def fn(E, sqv=sqv, c=c, b=b):
                        return E.matmul(ps[b][:, 0:T], onesm_b[:], sqv, start=(c == 0), stop=(c == 15))
                    S.op("pe", fn, reads=[Rs[si], Rones], writes=[P[b]])
            rs = scr[10][:, 0:T]
            S.op("act", lambda E: E.activation(rs, ps[b][:, 0:T], AF.Sqrt, bias=EPS), reads=[P[b]], writes=[Rs[10]])
            S.op("dve", lambda E: E.reciprocal(rs, rs), reads=[Rs[10]], writes=[Rs[10]])
            return rs, Rs[10]

        def norm_apply(tc, xs, Rxs, rs, Rrs, scale_fn, bias_fn, dst, Rdst_):
            T = tc.T
            for c in range(16):
                si = 11 + (c % 2)
                tmp = scr[si][:, 0:T]
                S.op("dve", lambda E, tmp=tmp, c=c: E.tensor_tensor(tmp, xs(c), rs, ALU.mult), reads=[Rxs[c], Rrs], writes=[Rs[si]])
                for (c0, n, sidx) in tc.segs:
                    if bias_fn is None:
                        S.op("act", lambda E, tmp=tmp, c=c, c0=c0, n=n, sidx=sidx: E.activation(
                            dst(c)[:, c0:c0 + n], tmp[:, c0:c0 + n], AF.Identity, scale=scale_fn(c, sidx)),
                            reads=[Rs[si], Rvec, Rmod], writes=[Rdst_[c]])
                    else:
                        S.op("act", lambda E, tmp=tmp, c=c, c0=c0, n=n, sidx=sidx: E.activation(
                            dst(c)[:, c0:c0 + n], tmp[:, c0:c0 + n], AF.Identity, scale=scale_fn(c, sidx), bias=bias_fn(c, sidx)),
                            reads=[Rs[si], Rvec, Rmod], writes=[Rdst_[c]])

        def proj_resid(tc, wd, wbase, src, Rsrc, nk, wn, gate_j, L, xs, Rxs, stats=False):
            T = tc.T
            bD = bank("D") if stats else None
            pend = []

            def flush():
                for (n_, sqv, si) in pend:
                    def fn(E, sqv=sqv, n_=n_):
                        return E.matmul(ps[bD][:, 0:T], onesm_b[:], sqv, start=(n_ == 0), stop=(n_ == 15))
                    S.op("pe", fn, reads=[Rs[si], Rones], writes=[P[bD]])
                del pend[:]
            for n in range(16):
                wt, wr = wload(wd[wbase + n], wn)
                wv = wt[:, 0:wn].rearrange("p (k n) -> p k n", k=nk)
                b = bank("C")
                mmg(ps[b][:, 0:T], [(wv[:, k, :], src(k)) for k in range(nk)], [wr] + [Rsrc[k] for k in range(nk)], P[b])
                flush()
                for (c0, nn, sidx) in tc.segs:
                    S.op("dve", lambda E, b=b, n=n, c0=c0, nn=nn, sidx=sidx: E.scalar_tensor_tensor(
                        xs(n)[:, c0:c0 + nn], ps[b][:, c0:c0 + nn], M_(L, gate_j, n, sidx), xs(n)[:, c0:c0 + nn],
                        ALU.mult, ALU.add), reads=[P[b], Rmod, Rxs[n]], writes=[Rxs[n]])
                if stats:
                    si = 8 + (n % 2)
                    sqv = scr[si][:].bitcast(BF16)[:, 0:T]
                    S.op("act", lambda E, sqv=sqv, n=n: E.activation(sqv, xs(n), AF.Square), reads=[Rxs[n]], writes=[Rs[si]])
                    pend.append((n, sqv, si))
            flush()
            if stats:
                tc.stats_bank = bD

        def ffn(tc, L, xs, Rxs, hns, Rhns, mixs, Rmixs):
            T = tc.T
            rs, Rrs = norm_stats(tc, xs, Rxs)
            norm_apply(tc, xs, Rxs, rs, Rrs, lambda c, s: M_(L, 4, c, s), lambda c, s: M_(L, 3, c, s), hns, Rhns)
            for g in range(NGRP):
                jj = 0
                while jj < GSZ:
                    pair = [jj, jj + 1] if (g == 0 and jj == 0) else [jj]
                    items, meta = [], []
                    for j2 in pair:
                        fc = g * GSZ + j2
                        w1t, w1r = wload(w_1[L * 44 + fc], 2048)
                        w3t, w3r = wload(w_3[L * 44 + fc], 2048)
                        w1v = w1t[:].rearrange("p (k n) -> p k n", k=16)
                        w3v = w3t[:].rearrange("p (k n) -> p k n", k=16)
                        ba, bb = bank("A"), bank("B")
                        items.append((ps[ba][:, 0:T], w1v, w1r, P[ba]))
                        items.append((ps[bb][:, 0:T], w3v, w3r, P[bb]))
                        meta.append((j2, fc, ba, bb))
                    if len(pair) == 2:
                        mm_kmajor(items, hns, Rhns, 16)
                    else:
                        for (oap, wv, wres, pres) in items:
                            mmg(oap, [(wv[:, k, :], hns(k)) for k in range(16)], [wres] + Rhns, pres)
                    for (j2, fc, ba, bb) in meta:
                        si = 13 + (fc % 2)
                        sa = scr[si][:, 0:T]
                        S.op("act", lambda E, sa=sa, ba=ba: E.activation(sa, ps[ba][:, 0:T], AF.Silu), reads=[P[ba]], writes=[Rs[si]])
                        S.op("dve", lambda E, sa=sa, bb=bb, j2=j2: E.tensor_tensor(mixs(j2), sa, ps[bb][:, 0:T], ALU.mult),
                             reads=[Rs[si], P[bb]], writes=[Rmixs[j2]])
                    jj += len(pair)
                proj_resid(tc, w_2, (L * 4 + g) * 16, mixs, Rmixs, GSZ, 1408, 5, L, xs, Rxs, stats=(g == NGRP - 1))

        def rg_layer(tc, L, xs, Rxs, hns, Rhns, mixs, Rmixs):
            T = tc.T
            nseg, SL = tc.nseg, tc.SL
            rs, Rrs = norm_stats(tc, xs, Rxs)
            norm_apply(tc, xs, Rxs, rs, Rrs, lambda c, s: M_(L, 1, c, s), lambda c, s: M_(L, 0, c, s), hns, Rhns)
            cst, Rcst = (cst_s, Rcsts) if tc.sample else (cst_p, Rcstp)
            hst, Rhst = (hst_s, Rhsts) if tc.sample else (hst_p, Rhstp)
            W3 = nseg * (SL + 3)

            def stage_a(blk):
                par = blk % 2
                pre0 = []
                if blk == 0:
                    items = []
                    for cc in range(2):
                        wit, wir = wload(w_in[L * 16 + cc], 2048)
                        wgt, wgr = wload(w_gate[L * 16 + cc], 2048)
                        ba, bb = bank("A"), bank("B")
                        items.append((ps[ba][:, 0:T], wit[:].rearrange("p (k n) -> p k n", k=16), wir, P[ba]))
                        items.append((ps[bb][:, 0:T], wgt[:].rearrange("p (k n) -> p k n", k=16), wgr, P[bb]))
                        pre0.append((ba, bb))
                    mm_kmajor(items, hns, Rhns, 16)
                for cc in range(2):
                    c = blk * 2 + cc
                    if blk != 0:
                        wit, wir = wload(w_in[L * 16 + c], 2048)
                        wgt, wgr = wload(w_gate[L * 16 + c], 2048)
                        wiv = wit[:].rearrange("p (k n) -> p k n", k=16)
                        wgv = wgt[:].rearrange("p (k n) -> p k n", k=16)
                    if blk == 0:
                        ba, bb = pre0[cc]
                    else:
                        ba, bb = bank("A"), bank("B")
                        mmg(ps[ba][:, 0:T], [(wiv[:, k, :], hns(k)) for k in range(16)], [wir] + Rhns, P[ba])
                        mmg(ps[bb][:, 0:T], [(wgv[:, k, :], hns(k)) for k in range(16)], [wgr] + Rhns, P[bb])
                    sxb, sxc, sg, sxcb = cc, 2 + 2 * par + cc, 6 + par, 8 + par
                    xb3 = scr[sxb][:, 0:W3].rearrange("p (s l) -> p s l", s=nseg)
                    xc = scr[sxc][:, 0:T]
                    xc3 = xc.rearrange("p (s l) -> p s l", s=nseg)
                    gb = scr[sg][:].bitcast(BF16)[:, cc * 512:cc * 512 + T]
                    xcb = scr[sxcb][:].bitcast(BF16)[:, cc * 512:cc * 512 + T]
                    pa3 = ps[ba][:, 0:T].rearrange("p (s l) -> p s l", s=nseg)
                    Rxb, Rxc, Rg, Rxcb = Rs[sxb], Rs[sxc], Rs[sg], Rs[sxcb]
                    if tc.sample:
                        S.op("dve", lambda E, xb3=xb3, c=c: E.tensor_copy(xb3[:, :, 0:3], conv_in[:, L, c]), reads=[Rcin], writes=[Rxb])
                    elif tc.j == 0:
                        S.op("dve", lambda E, xb3=xb3: E.memset(xb3[:, :, 0:3], 0.0), writes=[Rxb])
                    else:
                        S.op("dve", lambda E, xb3=xb3, c=c: E.tensor_copy(xb3[:, :, 0:3], cst[:, L, c]), reads=[Rcst], writes=[Rxb])
                    S.op("dve", lambda E, xb3=xb3, pa3=pa3: E.tensor_copy(xb3[:, :, 3:3 + SL], pa3), reads=[P[ba]], writes=[Rxb])
                    S.op("act", lambda E, xc=xc, ba=ba, c=c: E.activation(
                        xc, ps[ba][:, 0:T], AF.Identity, scale=V_("conv_w", (L * 4 + 3) * 16 + c), bias=V_("conv_b", L * 16 + c)),
                        reads=[P[ba], Rvec], writes=[Rxc])
                    for k in range(3):
                        S.op("dve", lambda E, xc3=xc3, xb3=xb3, k=k, c=c: E.scalar_tensor_tensor(
                            xc3, xb3[:, :, k:k + SL], V_("conv_w", (L * 4 + k) * 16 + c), xc3, ALU.mult, ALU.add),
                            reads=[Rxb, Rvec, Rxc], writes=[Rxc])
                    S.op("dve", lambda E, xcb=xcb, xc=xc: E.tensor_copy(xcb, xc), reads=[Rxc], writes=[Rxcb])
                    S.op("dve", lambda E, xb3=xb3, c=c: E.tensor_copy(cst[:, L, c], xb3[:, :, SL:SL + 3]), reads=[Rxb], writes=[Rcst])
                    S.op("act", lambda E, gb=gb, bb=bb: E.activation(gb, ps[bb][:, 0:T], AF.Gelu_apprx_tanh), reads=[P[bb]], writes=[Rg])

            def stage_b(blk):
                par = blk % 2
                wai, wair = wload(w_ai[L * 8 + blk], 1024)
                waiv = wai[:, 0:1024].rearrange("p (g k n) -> p g k n", g=2, k=2)
                sg, sxcb = 6 + par, 8 + par
                xcbs = [scr[sxcb][:].bitcast(BF16)[:, k * 512:k * 512 + T] for k in range(2)]
                i_, h_ = scr[12][:, 0:T], scr[15][:, 0:T]
                Ri, Rh = Rs[12], Rs[15]
                for cc in range(2):
                    c = blk * 2 + cc
                    sxc = 2 + 2 * par + cc
                    xc = scr[sxc][:, 0:T]
                    Rxc = Rs[sxc]
                    r_, a_ = scr[10 + cc][:, 0:T], scr[13 + cc][:, 0:T]
                    Rr, Ra = Rs[10 + cc], Rs[13 + cc]
                    bc, bd = bank("C"), bank("D")
                    mmg(ps[bc][:, 0:T], [(waiv[:, 0, k, cc * 128:(cc + 1) * 128], xcbs[k]) for k in range(2)], [wair, Rs[sxcb]], P[bc])
                    mmg(ps[bd][:, 0:T], [(waiv[:, 1, k, cc * 128:(cc + 1) * 128], xcbs[k]) for k in range(2)], [wair, Rs[sxcb]], P[bd])
                    S.op("act", lambda E, r_=r_, bc=bc, c=c: E.activation(r_, ps[bc][:, 0:T], AF.Tanh, scale=0.5, bias=hb[:, 0, L * 16 + c:L * 16 + c + 1]), reads=[P[bc], Rkc8], writes=[Rr])
                    S.op("act", lambda E, bd=bd, c=c: E.activation(i_, ps[bd][:, 0:T], AF.Tanh, scale=0.5, bias=hb[:, 1, L * 16 + c:L * 16 + c + 1]), reads=[P[bd], Rkc8], writes=[Ri])
                    S.op("act", lambda E, a_=a_, r_=r_, c=c: E.activation(a_, r_, AF.Exp, scale=kc8[:, 0, L, c:c + 1], bias=kc8[:, 0, L, c:c + 1]), reads=[Rr, Rkc8], writes=[Ra])
                    S.op("act", lambda E, r_=r_, c=c: E.activation(r_, r_, AF.Exp, scale=kc8[:, 1, L, c:c + 1], bias=kc8[:, 1, L, c:c + 1]), reads=[Rr, Rkc8], writes=[Rr])
                    S.op("dve", lambda E, xc=xc: E.scalar_tensor_tensor(xc, i_, 1.0, xc, ALU.add, ALU.mult), reads=[Ri, Rxc], writes=[Rxc])
                for cc in range(2):
                    r_ = scr[10 + cc][:, 0:T]
                    Rr = Rs[10 + cc]
                    S.op("act", lambda E, r_=r_: E.activation(r_, r_, AF.Sqrt, scale=-0.25, bias=0.25), reads=[Rr], writes=[Rr])
                for cc in range(2):
                    c = blk * 2 + cc
                    sxc = 2 + 2 * par + cc
                    xc = scr[sxc][:, 0:T]
                    Rxc = Rs[sxc]
                    gb = scr[sg][:].bitcast(BF16)[:, cc * 512:cc * 512 + T]
                    Rg = Rs[sg]
                    r_, a_ = scr[10 + cc][:, 0:T], scr[13 + cc][:, 0:T]
                    Rr, Ra = Rs[10 + cc], Rs[13 + cc]
                    if (not tc.sample) and tc.j == 0:
                        S.op("dve", lambda E, r_=r_: E.memset(r_[:, 0:1], 0.5), reads=[Rr], writes=[Rr])
                    S.op("dve", lambda E, xc=xc, r_=r_: E.tensor_tensor(xc, xc, r_, ALU.mult), reads=[Rxc, Rr], writes=[Rxc])
                    for s_ in range(nseg):
                        if tc.sample:
                            init = rnn_in[:, L, c, s_:s_ + 1]
                            rd = [Rrin]
                        elif tc.j == 0:
                            init = 0.0
                            rd = []
                        else:
                            init = hst[:, L, c, 0:1]
                            rd = [Rhst]
                        S.op("dve", lambda E, a_=a_, xc=xc, s_=s_, init=init: E.tensor_tensor_scan(
                            h_[:, s_ * SL:(s_ + 1) * SL], a_[:, s_ * SL:(s_ + 1) * SL], xc[:, s_ * SL:(s_ + 1) * SL], init, ALU.mult, ALU.add),
                            reads=[Ra, Rxc] + rd, writes=[Rh])
                    h3 = h_.rearrange("p (s l) -> p s l", s=nseg)
                    S.op("dve", lambda E, h3=h3, c=c: E.tensor_copy(hst[:, L, c], h3[:, :, SL - 1]), reads=[Rh], writes=[Rhst])
                    S.op("dve", lambda E, gb=gb, c=c: E.tensor_tensor(mixs(c), h_, gb, ALU.mult), reads=[Rh, Rg], writes=[Rmixs[c]])

            stage_a(0)
            for blk in range(1, 8):
                stage_a(blk)
                stage_b(blk - 1)
            stage_b(7)
            proj_resid(tc, w_out, L * 16, mixs, Rmixs, 16, 2048, 2, L, xs, Rxs, stats=True)

        def load_expD(bl):
            for q4 in range(4):
                i = w_rr[0]
                w_rr[0] = (i + 1) % NW
                wf = wsl[i][:].bitcast(F32)
                S.op("sp", lambda E, wf=wf, q4=q4: E.dma_start(out=wf, in_=dtab_d[bl, :, q4 * 1024:(q4 + 1) * 1024]), writes=[Rw[i]], chan=Cw[i])
                S.op("act", lambda E, wf=wf, q4=q4: E.activation(
                    expD[:, q4 * 4:(q4 + 1) * 4].rearrange("p h t q -> p (h t q)"), wf, AF.Exp), reads=[Rw[i]], writes=[RexpD])
            S.op("dve", lambda E: E.memset(expD[64:128, :, 1, 0:64], 0.0), reads=[RexpD], writes=[RexpD])

        def attn_run(groups, bl):
            ng = len(groups)
            state = {}

            def qk_exp(k):
                g = groups[k]
                st_ = k % 3
                b0, b1 = 2 * st_, 2 * st_ + 1
                nq = g["nq"]
                qap, qres = g["q"]
                h = g["h"]
                pt = scr[0 + st_][:].bitcast(BF16)
                ev = scr[3 + st_]
                Rpt, Rev = Rs[0 + st_], Rs[3 + st_]
                keys = g["keys"]
                far = [kx for kx in keys if kx[0] <= 2]
                nd = [kx for kx in keys if kx[0] >= 3]
                allk = []
                for kx in keys:
                    allk += kx[5]

                def sps_of(pos, nk):
                    if pos <= 2:
                        return ps[b0][0:nk, pos * 128:pos * 128 + nq]
                    return ps[b1][0:nk, (pos - 3) * 128:(pos - 3) * 128 + nq]

                def fn(E):
                    ins = None
                    for (pos, KT, Vv, nk, kind, kres) in keys:
                        ins = E.matmul(sps_of(pos, nk), KT, qap, start=True, stop=True)
                    return ins
                wr = ([P[b0]] if far else []) + ([P[b1]] if nd else [])
                S.op("pe", fn, reads=allk + qres, writes=wr)
                if far:
                    lo = far[0][0]
                    src = ps[b0][:, 0:384].rearrange("p (i q) -> p i q", q=128)[:, lo:3, 0:nq]
                    dst = pt[:, 0:384].rearrange("p (i q) -> p i q", q=128)[:, lo:3, 0:nq]
                    S.op("act", lambda E: E.activation(dst, src, AF.Exp, bias=cfar[:, bl * 16 + h:bl * 16 + h + 1]),
                         reads=[P[b0], Rcfar], writes=[Rpt])
                    if far[0][4] == "farmask":
                        S.op("pool", lambda E: E.memset(pt[0:64, 64:128], 0.0), reads=[Rpt], writes=[Rpt])
                if len(nd) == 2 and nd[0][3] == 128 and nd[1][3] == 128 and nq == 128:
                    S.op("act", lambda E: E.activation(ev[:, 0:256], ps[b1][:, 0:256], AF.Exp), reads=[P[b1]], writes=[Rev])
                    S.op("pool", lambda E: E.tensor_tensor(pt[:, 384:640], ev[:, 0:256], expD[:, h].rearrange("p t q -> p (t q)"), ALU.mult),
                         reads=[Rev, RexpD], writes=[Rpt])
                else:
                    for (pos, KT, Vv, nk, kind, kres) in nd:
                        e = ev[0:nk, (pos - 3) * 128:(pos - 3) * 128 + nq]
                        sps = sps_of(pos, nk)
                        pti = pt[0:nk, pos * 128:pos * 128 + nq]
                        S.op("act", lambda E, e=e, sps=sps: E.activation(e, sps, AF.Exp), reads=[P[b1]], writes=[Rev])
                        S.op("pool", lambda E, pti=pti, e=e, nk=nk, pos=pos: E.tensor_tensor(pti, e, expD[0:nk, h, pos - 3, 0:nq], ALU.mult),
                             reads=[Rev, RexpD], writes=[Rpt])
                ptiles = [(pt[0:nk, pos * 128:pos * 128 + nq], Vv, nk) for (pos, KT, Vv, nk, kind, kres) in keys]
                state[k] = (ptiles, allk, b1, st_)

            def pv_norm(k):
                g = groups[k]
                ptiles, allk, b1, st_ = state.pop(k)
                nq = g["nq"]
                Rpt, Rev = Rs[0 + st_], Rs[3 + st_]
                ev = scr[3 + st_]
                ops_ = ps[b1][:, 384:384 + nq]
                dps = ps[b1][:, 256:256 + nq]
                mmg(ops_, [(Vv, pti) for (pti, Vv, nk) in ptiles], [Rpt] + allk, P[b1])
                mmg(dps, [(ones_b[0:nk, :], pti) for (pti, Vv, nk) in ptiles], [Rpt, Rones], P[b1])
                rd = ev[:, 256:256 + nq]
                S.op("dve", lambda E: E.reciprocal(rd, dps), reads=[P[b1]], writes=[Rev])
                oap, ores = g["out"]
                S.op("dve", lambda E: E.tensor_tensor(oap, ops_, rd, ALU.mult), reads=[P[b1], Rev], writes=[ores])

            for k in range(ng + 1):
                if k < ng:
                    if groups[k].get("pre") is not None:
                        groups[k]["pre"]()
                    qk_exp(k)
                if k >= 1:
                    pv_norm(k - 1)

        def q_proj(tc, bl, h, hns, Rhns, dst_ap, dst_res):
            T = tc.T
            wt, wr = wload(w_q[bl * 16 + h], 2048)
            wv = wt[:].rearrange("p (k n) -> p k n", k=16)
            b = bank("D")
            mmg(ps[b][:, 0:T], [(wv[:, k, :], hns(k)) for k in range(16)], [wr] + Rhns, P[b])
            S.op("act", lambda E: E.activation(dst_ap, ps[b][:, 0:T], AF.Copy, scale=QSCALE), reads=[P[b]], writes=[dst_res])

        def kv_proj(tc, xs, Rxs, hns, Rhns, rs, Rrs, KT_dst, RKT, V_dst_fn, k_out_fn, v_out_fn):
            T = tc.T
            norm_apply(tc, xs, Rxs, rs, Rrs, lambda c, s: V_("g_kv", c), None, hns, Rhns)
            kb0 = []
            items = []
            for h in range(4):
                wt, wr = wload(w_k[h], 2048)
                b = bank("A") if h < 2 else bank("B")
                items.append((ps[b][:, 0:T], wt[:].rearrange("p (k n) -> p k n", k=16), wr, P[b]))
                kb0.append(b)
            mm_kmajor(items, hns, Rhns, 16)
            for h in range(16):
                if h < 4:
                    b = kb0[h]
                else:
                    wt, wr = wload(w_k[h], 2048)
                    wv = wt[:].rearrange("p (k n) -> p k n", k=16)
                    b = bank("A")
                    mmg(ps[b][:, 0:T], [(wv[:, k, :], hns(k)) for k in range(16)], [wr] + Rhns, P[b])
                S.op("act", lambda E, h=h, b=b: E.activation(KT_dst(h), ps[b][:, 0:T], AF.Copy), reads=[P[b]], writes=[RKT[h]])
                if k_out_fn is not None:
                    si = 13 + (h % 2)
                    stg = scr[si][:, 0:T]
                    S.op("dve", lambda E, stg=stg, b=b: E.tensor_copy(stg, ps[b][:, 0:T]), reads=[P[b]], writes=[Rs[si]])
                    S.op("sp", lambda E, stg=stg, h=h: E.dma_start(out=k_out_fn(h), in_=stg), reads=[Rs[si]], chan=c_out[si - 13])
            for n in range(16):
                wt, wr = wload(w_v[n], 2048)
                wv = wt[:].rearrange("p (k n) -> p k n", k=16)
                for (tb, ntok, vdst, vres, vout) in V_dst_fn(n):
                    b = bank("B")
                    mmg(ps[b][0:ntok, 0:128], [(hns(k)[:, tb:tb + ntok], wv[:, k, :]) for k in range(16)], [wr] + Rhns, P[b])
                    S.op("act", lambda E, b=b, ntok=ntok, vdst=vdst: E.activation(vdst, ps[b][0:ntok, 0:128], AF.Copy), reads=[P[b]], writes=[vres])
                    if vout is not None:
                        si = 13 + (n % 2)
                        stg = scr[si][0:ntok, 256:384]
                        S.op("dve", lambda E, stg=stg, b=b, ntok=ntok: E.tensor_copy(stg, ps[b][0:ntok, 0:128]), reads=[P[b]], writes=[Rs[si]])
                        S.op("sp", lambda E, stg=stg, vout=vout: E.dma_start(out=vout, in_=stg), reads=[Rs[si]], chan=c_out[si - 13])

        def final_norm(tc, xs, Rxs, y_out_fn):
            T = tc.T
            rs, Rrs = norm_stats(tc, xs, Rxs)
            for c in range(16):
                si = 11 + (c % 2)
                tmp = scr[si][:, 0:T]
                S.op("dve", lambda E, tmp=tmp, c=c: E.tensor_tensor(tmp, xs(c), rs, ALU.mult), reads=[Rxs[c], Rrs], writes=[Rs[si]])
                so = 13 + (c % 2)
                stg = scr[so][:, 0:T]
                S.op("act", lambda E, tmp=tmp, stg=stg, c=c: E.activation(stg, tmp, AF.Identity, scale=V_("g_final", c)), reads=[Rs[si], Rvec], writes=[Rs[so]])
                S.op("sp", lambda E, stg=stg, c=c: E.dma_start(out=y_out_fn(c), in_=stg), reads=[Rs[so]], chan=c_out[so - 13])

        def state_out(conv_dst, rnn_dst, cst, Rcst, hst, Rhst):
            S.op("sp", lambda E: E.dma_start(out=conv_dst, in_=cst[:].rearrange("p a b c d -> p (a b c d)")), reads=[Rcst], chan=c_out[2])
            S.op("sp", lambda E: E.dma_start(out=rnn_dst, in_=hst[:].rearrange("p a b c -> p (a b c)")), reads=[Rhst], chan=c_out[3])

        for g8 in range(8):
            seq, j = g8 // 4, g8 % 4
            tc = Tile()
            tc.T, tc.segs, tc.j, tc.sample, tc.nseg, tc.SL = 512, [(0, 512, seq)], j, False, 1, 512
            xs = lambda c: x[:, c, :]
            hns = lambda c: hn[:, c, :]
            mixs = lambda c: mix[:, c, :]
            cur, prv = g8 % 2, (g8 + 1) % 2
            S.op("sp", lambda E, g8=g8: E.dma_start(out=x[:].rearrange("p c t -> p (c t)"), in_=xT_p[g8]), writes=Rx, chan=c_x)
            for L in range(2):
                rg_layer(tc, L, xs, Rx, hns, Rhn, mixs, Rmix)
                ffn(tc, L, xs, Rx, hns, Rhn, mixs, Rmix)
            last = (j == 3)
            if last:
                state_out(conv_p[seq], rnn_p[seq], cst_p, Rcstp, hst_p, Rhstp)
            rs, Rrs = norm_stats(tc, xs, Rx)

            def V_dst_fn(n, cur=cur, seq=seq, last=last):
                return [(tb * 128, 128, bandV[cur][:, tb, n * 128:(n + 1) * 128], RbV[cur][tb],
                         (v_p[seq, tb * 128:(tb + 1) * 128, n * 128:(n + 1) * 128] if last else None)) for tb in range(4)]
            kv_proj(tc, xs, Rx, hns, Rhn, rs, Rrs, lambda h, cur=cur: bandK[cur][:, h, :], RbK[cur], V_dst_fn,
                    (lambda h, seq=seq: kT_p[seq, :, h, :]) if last else None, None)
            for bl in range(2):
                L = 2 + bl
                if bl == 1:
                    rs, Rrs = norm_stats(tc, xs, Rx)
                norm_apply(tc, xs, Rx, rs, Rrs, lambda c, s, L=L: M_(L, 1, c, s), lambda c, s, L=L: M_(L, 0, c, s), hns, Rhn)
                load_expD(bl)
                groups = []
                for h in range(16):
                    for qg in range(4):
                        keys = []
                        for i in range(5):
                            kb = qg - 4 + i
                            if kb < 0:
                                if j == 0:
                                    continue
                                buf, blk = prv, kb + 4
                            else:
                                buf, blk = cur, kb
                            kind = ("farmask", "far", "far", "near", "diag")[i]
                            keys.append((i, bandK[buf][:, h, blk * 128:(blk + 1) * 128], bandV[buf][:, blk, h * 128:(h + 1) * 128],
                                         128, kind, [RbK[buf][h], RbV[buf][blk]]))
                        qslot = 6 + (h % 2)
                        qap = scr[qslot][:].bitcast(BF16)[:, qg * 128:(qg + 1) * 128]
                        gd = dict(q=(qap, [Rs[qslot]]), nq=128, keys=keys, out=(mix[:, h, qg * 128:(qg + 1) * 128], Rmix[h]), h=h)
                        if qg == 1 and 1 <= h and h + 1 < 16:
                            gd["pre"] = (lambda h=h, bl=bl, tc=tc, hns=hns: q_proj(
                                tc, bl, h + 1, hns, Rhn, scr[6 + ((h + 1) % 2)][:].bitcast(BF16)[:, 0:512], Rs[6 + ((h + 1) % 2)]))
                        groups.append(gd)
                items, qb = [], []
                for h in range(2):
                    wt, wr = wload(w_q[bl * 16 + h], 2048)
                    b = bank("D")
                    items.append((ps[b][:, 0:512], wt[:].rearrange("p (k n) -> p k n", k=16), wr, P[b]))
                    qb.append(b)
                mm_kmajor(items, hns, Rhn, 16)
                for h in range(2):
                    S.op("act", lambda E, h=h, b=qb[h]: E.activation(scr[6 + h][:].bitcast(BF16)[:, 0:512], ps[b][:, 0:512], AF.Copy, scale=QSCALE),
                         reads=[P[qb[h]]], writes=[Rs[6 + h]])
                attn_run(groups, bl)
                proj_resid(tc, w_o, bl * 16, mixs, Rmix, 16, 2048, 2, L, xs, Rx, stats=True)
                ffn(tc, L, xs, Rx, hns, Rhn, mixs, Rmix)
            final_norm(tc, xs, Rx, lambda c, g8=g8: yT_p[g8, :, c, :])

        S.barrier()
        tc = Tile()
        tc.T, tc.segs, tc.j, tc.sample, tc.nseg, tc.SL = 256, [(64 * i, 64, 2 + i) for i in range(4)], None, True, 4, 64
        xs = lambda c: x[:, c, 0:256]
        hns = lambda c: hn[:, c, 0:256]
        mixs = lambda c: mix[:, c, 0:256]
        Rx2 = [Res("x2_%d" % c) for c in range(16)]
        Rhn2 = [Res("hn2_%d" % c) for c in range(16)]
        Rmix2 = [Res("mix2_%d" % c) for c in range(16)]
        RKn = [Res("Kn%d" % h) for h in range(16)]
        RVn = [Res("Vn%d" % h) for h in range(16)]
        Rq = [Res("q%d" % h) for h in range(16)]
        RbK2 = [Res("bK2_%d" % i) for i in range(2)]
        RbV2 = [Res("bV2_%d" % i) for i in range(2)]
        KTn = lambda h: hn[:, h, 256:512]
        Vn = lambda h: x[0:64, h, 256:512].bitcast(BF16).rearrange("p (s d) -> p s d", s=4)
        qall = lambda h: mix[:, h, 256:512]
        S.op("sp", lambda E: E.dma_start(out=x[:, :, 0:256], in_=xT_s), writes=Rx2, chan=c_x)
        for L in range(2):
            rg_layer(tc, L, xs, Rx2, hns, Rhn2, mixs, Rmix2)
            ffn(tc, L, xs, Rx2, hns, Rhn2, mixs, Rmix2)
        state_out(conv_s, rnn_s, cst_s, Rcsts, hst_s, Rhsts)
        rs, Rrs = norm_stats(tc, xs, Rx2)

        def V_dst_fn_s(n):
            return [(s * 64, 64, Vn(n)[:, s, :], RVn[n], v_s[s, :, n * 128:(n + 1) * 128]) for s in range(4)]
        kv_proj(tc, xs, Rx2, hns, Rhn2, rs, Rrs, KTn, RKn, V_dst_fn_s, lambda h: kT_s[:, h, :], None)
        for bl in range(2):
            L = 2 + bl
            if bl == 1:
                rs, Rrs = norm_stats(tc, xs, Rx2)
            norm_apply(tc, xs, Rx2, rs, Rrs, lambda c, s, L=L: M_(L, 1, c, s), lambda c, s, L=L: M_(L, 0, c, s), hns, Rhn2)
            load_expD(bl)
            for h in range(16):
                q_proj(tc, bl, h, hns, Rhn2, qall(h), Rq[h])

            def load_cache(s):
                buf = s % 2
                S.op("pool", lambda E: E.dma_start(out=bandK[buf][:].rearrange("p h t -> p (h t)"), in_=kTc_d[s]), writes=[RbK2[buf]], chan=c_bk[buf])
                S.op("pool", lambda E: E.dma_start(out=bandV[buf][:].rearrange("p b f -> p (b f)"), in_=vc_d[s]), writes=[RbV2[buf]], chan=c_bv[buf])
            load_cache(0)
            load_cache(1)
            groups = []
            for s in range(4):
                buf = s % 2
                for h in range(16):
                    keys = []
                    for i in range(4):
                        kind = ("far", "far", "far", "near")[i]
                        keys.append((i, bandK[buf][:, h, i * 128:(i + 1) * 128], bandV[buf][:, i, h * 128:(h + 1) * 128], 128, kind,
                                     [RbK2[buf], RbV2[buf]]))
                    keys.append((4, KTn(h)[:, s * 64:(s + 1) * 64], Vn(h)[:, s, :], 64, "diag", [RKn[h], RVn[h]]))
                    gd = dict(q=(qall(h)[:, s * 64:(s + 1) * 64], [Rq[h]]), nq=64, keys=keys,
                              out=(mix[:, h, s * 64:(s + 1) * 64], Rmix2[h]), h=h)
                    if h == 2 and s >= 1 and s + 1 < 4:
                        gd["pre"] = (lambda s=s: load_cache(s + 1))
                    groups.append(gd)
            attn_run(groups, bl)
            proj_resid(tc, w_o, bl * 16, mixs, Rmix2, 16, 2048, 2, L, xs, Rx2, stats=True)
            ffn(tc, L, xs, Rx2, hns, Rhn2, mixs, Rmix2)
        final_norm(tc, xs, Rx2, lambda c: yT_s[:, c, :])

        S.final_wait("sp")
        with nc.Block() as block:
            S.emit(block)
    return nc


def _fm(v):
    v = np.asarray(v, np.float32)
    lead = v.shape[:-1]
    return np.moveaxis(v.reshape(lead + (16, 128)), -1, 0)


def _tile_w(W, kc):
    K, N = W.shape
    t = W.reshape(kc, 128, N // 128, 128).transpose(2, 1, 0, 3)
    return np.ascontiguousarray(t).reshape(N // 128, 128, kc * 128)


_NC_CACHE = {}


def kernel(x_prompt, x_sample, c_prompt, c_sample, state_conv, state_rnn, cache_k, cache_v,
           ada_w, ada_b, g_mix, g_ffn, rg_w_in, rg_w_gate, rg_conv_w, rg_conv_b, rg_w_a, rg_b_a,
           rg_w_i, rg_b_i, rg_lambda, rg_w_out, g_kv, w_k, w_v, w_q, w_o, rel_bias,
           ffn_w1, ffn_w3, ffn_w2, g_final):
    f32 = np.float32
    A = lambda a: np.asarray(a, f32)
    NCORE = 8
    vec_parts = {
        "g_mix": _fm(A(g_mix)), "g_ffn": _fm(A(g_ffn)), "conv_w": _fm(A(rg_conv_w)), "conv_b": _fm(A(rg_conv_b)),
        "b_a": _fm(A(rg_b_a)), "b_i": _fm(A(rg_b_i)), "lam": _fm(A(rg_lambda)), "g_kv": _fm(A(g_kv)),
        "g_final": _fm(A(g_final)), "ada_b": _fm(A(ada_b).reshape(4, 6, 2048)),
    }
    vecs = np.concatenate([vec_parts[n].reshape(128, -1) for n, _ in VEC_SPEC], axis=1)
    assert vecs.shape == (128, NV)
    vecs = np.ascontiguousarray(vecs, f32)
    w_ada_t = np.concatenate([_tile_w(A(ada_w)[L], 16) for L in range(4)], axis=0)
    w_in_t = np.concatenate([_tile_w(A(rg_w_in)[L], 16) for L in range(2)], axis=0)
    w_gate_t = np.concatenate([_tile_w(A(rg_w_gate)[L], 16) for L in range(2)], axis=0)
    w_out_t = np.concatenate([_tile_w(A(rg_w_out)[L], 16) for L in range(2)], axis=0)
    wa, wi = A(rg_w_a), A(rg_w_i)
    w_ai_t = np.zeros((16, 128, 2, 2, 256), f32)
    for L in range(2):
        for b in range(8):
            w_ai_t[L * 8 + b, :, 0] = wa[L, b].reshape(2, 128, 256).transpose(1, 0, 2)
            w_ai_t[L * 8 + b, :, 1] = wi[L, b].reshape(2, 128, 256).transpose(1, 0, 2)
    w_ai_t = w_ai_t.reshape(16, 128, 1024)
    w_k_t = _tile_w(A(w_k), 16)
    w_v_t = _tile_w(A(w_v), 16)
    w_q_t = np.concatenate([_tile_w(A(w_q)[L], 16) for L in range(2)], axis=0)
    w_o_t = np.concatenate([_tile_w(A(w_o)[L], 16) for L in range(2)], axis=0)
    w_1_t = np.concatenate([_tile_w(A(ffn_w1)[L], 16) for L in range(4)], axis=0)
    w_3_t = np.concatenate([_tile_w(A(ffn_w3)[L], 16) for L in range(4)], axis=0)
    w2 = A(ffn_w2)
    w_2_t = np.ascontiguousarray(w2.reshape(4, 4, 11, 128, 16, 128).transpose(0, 1, 4, 3, 2, 5)).reshape(256, 128, 1408)
    rb = A(rel_bias)
    kk = np.arange(128)[:, None]
    qq = np.arange(128)[None, :]
    i0 = qq - kk + 128
    i1 = np.minimum(qq - kk + 256, 256)
    dtab = np.zeros((2, 128, 16, 2, 128), f32)
    for bl in range(2):
        dtab[bl, :, :, 1, :] = rb[bl][i0].transpose(0, 2, 1)
        dtab[bl, :, :, 0, :] = rb[bl][i1].transpose(0, 2, 1)
    dtab = dtab.reshape(2, 128, 4096)
    cfar = np.ascontiguousarray(np.broadcast_to(rb[:, 256, :].reshape(1, 32), (128, 32)), f32)

    xp, xsm = A(x_prompt), A(x_sample)
    cp, csm = A(c_prompt), A(c_sample)
    sc, sr = A(state_conv), A(state_rnn)
    ck, cv = A(cache_k), A(cache_v)
    shared = dict(vecs=vecs, dtab=dtab, cfar=cfar, w_ada=w_ada_t, w_in=w_in_t, w_gate=w_gate_t, w_ai=w_ai_t,
                  w_out=w_out_t, w_k=w_k_t, w_v=w_v_t, w_q=w_q_t, w_o=w_o_t, w_1=w_1_t, w_3=w_3_t, w_2=w_2_t)
    in_maps = []
    for core in range(NCORE):
        m = dict(shared)
        xt = xp[core * 2:(core + 1) * 2].reshape(8, 512, 16, 128).transpose(0, 3, 2, 1)
        m["xT_p"] = np.ascontiguousarray(xt).reshape(8, 128, 8192)
        xs_ = xsm[core * 4:(core + 1) * 4].reshape(256, 16, 128).transpose(2, 1, 0)
        m["xT_s"] = np.ascontiguousarray(xs_)
        c6 = np.concatenate([cp[core * 2:(core + 1) * 2], csm[core * 4:(core + 1) * 4]], axis=0)
        m["cT"] = np.ascontiguousarray(c6.reshape(6, 16, 128).transpose(2, 1, 0)).reshape(128, 96)
        sc4 = sc[:, core * 4:(core + 1) * 4]
        m["conv_in"] = np.ascontiguousarray(sc4.reshape(2, 4, 3, 16, 128).transpose(4, 0, 3, 1, 2)).reshape(128, 384)
        sr4 = sr[:, core * 4:(core + 1) * 4]
        m["rnn_in"] = np.ascontiguousarray(sr4.reshape(2, 4, 16, 128).transpose(3, 0, 2, 1)).reshape(128, 128)
        ck4 = ck[core * 4:(core + 1) * 4]
        m["kTc"] = np.ascontiguousarray(ck4.transpose(0, 3, 2, 1)).reshape(4, 128, 8192)
        cv4 = cv[core * 4:(core + 1) * 4].reshape(4, 4, 128, 2048)
        m["vc"] = np.ascontiguousarray(cv4.transpose(0, 2, 1, 3)).reshape(4, 128, 8192)
        in_maps.append(m)

    if "nc" not in _NC_CACHE:
        _NC_CACHE["nc"] = build_nc()
    nc = _NC_CACHE["nc"]
    res = run_bass_kernel_spmd(nc, in_maps, core_ids=list(range(NCORE)))
    R = res.results

    y_prompt = np.zeros((16, 2048, 2048), f32)
    y_sample = np.zeros((32, 64, 2048), f32)
    conv_prompt = np.zeros((2, 16, 3, 2048), f32)
    rnn_prompt = np.zeros((2, 16, 2048), f32)
    k_prompt = np.zeros((16, 512, 16, 128), f32)
    v_prompt = np.zeros((16, 512, 16, 128), f32)
    conv_sample = np.zeros((2, 32, 3, 2048), f32)
    rnn_sample = np.zeros((2, 32, 2048), f32)
    k_sample = np.zeros((32, 64, 16, 128), f32)
    v_sample = np.zeros((32, 64, 16, 128), f32)
    for core in range(NCORE):
        r = R[core]
        yp = np.asarray(r["yT_p"]).reshape(2, 4, 128, 16, 512)
        y_prompt[core * 2:(core + 1) * 2] = yp.transpose(0, 1, 4, 3, 2).reshape(2, 2048, 2048)
        ys = np.asarray(r["yT_s"]).reshape(128, 16, 4, 64)
        y_sample[core * 4:(core + 1) * 4] = ys.transpose(2, 3, 1, 0).reshape(4, 64, 2048)
        cpo = np.asarray(r["conv_p"]).reshape(2, 128, 2, 16, 3)
        conv_prompt[:, core * 2:(core + 1) * 2] = cpo.transpose(2, 0, 4, 3, 1).reshape(2, 2, 3, 2048)
        rpo = np.asarray(r["rnn_p"]).reshape(2, 128, 2, 16)
        rnn_prompt[:, core * 2:(core + 1) * 2] = rpo.transpose(2, 0, 3, 1).reshape(2, 2, 2048)
        cso = np.asarray(r["conv_s"]).reshape(128, 2, 16, 4, 3)
        conv_sample[:, core * 4:(core + 1) * 4] = cso.transpose(1, 3, 4, 2, 0).reshape(2, 4, 3, 2048)
        rso = np.asarray(r["rnn_s"]).reshape(128, 2, 16, 4)
        rnn_sample[:, core * 4:(core + 1) * 4] = rso.transpose(1, 3, 2, 0).reshape(2, 4, 2048)
        kp = np.asarray(r["kT_p"]).reshape(2, 128, 16, 512)
        k_prompt[core * 2:(core + 1) * 2] = kp.transpose(0, 3, 2, 1)
        v_prompt[core * 2:(core + 1) * 2] = np.asarray(r["v_p"]).reshape(2, 512, 16, 128)
        ks = np.asarray(r["kT_s"]).reshape(128, 16, 4, 64)
        k_sample[core * 4:(core + 1) * 4] = ks.transpose(2, 3, 1, 0)
        v_sample[core * 4:(core + 1) * 4] = np.asarray(r["v_s"]).reshape(4, 64, 16, 128)
    return (y_prompt, y_sample, conv_prompt, rnn_prompt, k_prompt, v_prompt,
            conv_sample, rnn_sample, k_sample, v_sample)
```

```python
import contextlib
import numpy as np
import concourse.bass as bass
import concourse.mybir as mybir
from concourse.bass_utils import run_bass_kernel_spmd

F32 = mybir.dt.float32
BF16 = mybir.dt.bfloat16
AF = mybir.ActivationFunctionType
ALU = mybir.AluOpType

D = 2048
NCH = 16
FF = 5632
NFC = 44
NGRP = 4
GSZ = 11
DEPTH = 4
EPS = 1e-6
NW = 5
NS = 16
QSCALE = 128 ** -0.5


class Eng:
    def __init__(self, name, sem, inc, kind):
        self.name, self.sem, self.inc, self.kind = name, sem, inc, kind
        self.cnt = 0
        self.seen = {}
        self.prog = []


class Res:
    __slots__ = ("name", "w", "r", "excl")

    def __init__(self, name, excl=False):
        self.name, self.w, self.r, self.excl = name, None, {}, excl


class Sched:
    def __init__(self, nc, stack):
        self.nc = nc
        self.stack = stack
        self.engs = {}
        for name in ("pe", "act", "dve", "pool", "sp"):
            sem = stack.enter_context(nc.semaphore("s_" + name))
            self.engs[name] = Eng(name, sem, 1, name)
        self.nchan = 0
        self.chans = []

    def chan(self):
        self.nchan += 1
        sem = self.stack.enter_context(self.nc.semaphore("c_%d" % self.nchan))
        e = Eng("ch%d" % self.nchan, sem, 16, "chan")
        self.chans.append(e)
        return e

    def op(self, eng, fn, reads=(), writes=(), chan=None):
        eng = self.engs[eng]
        comp = chan or eng
        deps = {}
        for r in reads:
            if r.w is not None:
                e, c = r.w
                if c > deps.get(e, 0):
                    deps[e] = c
            if r.excl:
                for e, c in r.r.items():
                    if e is not comp and c > deps.get(e, 0):
                        deps[e] = c
        for w in writes:
            if w.w is not None:
                e, c = w.w
                if c > deps.get(e, 0):
                    deps[e] = c
            for e, c in w.r.items():
                if (e is not comp) and c > deps.get(e, 0):
                    deps[e] = c
        for e, c in deps.items():
            if e is eng and eng.kind == "pe":
                continue
            if eng.seen.get(e, 0) >= c:
                continue
            eng.seen[e] = c
            eng.prog.append(("w", e.sem, c))
        comp.cnt += comp.inc
        eng.prog.append(("i", fn, comp.sem, comp.inc))
        for w in writes:
            w.w = (comp, comp.cnt)
            w.r = {}
        for r in reads:
            if r not in writes:
                r.r[comp] = comp.cnt

    def barrier(self):
        alle = list(self.engs.values()) + self.chans
        for eng in self.engs.values():
            for e in alle:
                if e is eng or e.cnt == 0:
                    continue
                if eng.seen.get(e, 0) >= e.cnt:
                    continue
                eng.seen[e] = e.cnt
                eng.prog.append(("w", e.sem, e.cnt))

    def final_wait(self, eng):
        eng = self.engs[eng]
        for e in self.chans:
            if e.cnt > 0:
                eng.prog.append(("w", e.sem, e.cnt))

    def emit(self, block):
        def run(prog):
            def f(E):
                for it in prog:
                    if it[0] == "w":
                        E.wait_ge(it[1], it[2])
                    else:
                        it[1](E).then_inc(it[2], it[3])
            return f
        block.tensor(run(self.engs["pe"].prog))
        block.scalar(run(self.engs["act"].prog))
        block.vector(run(self.engs["dve"].prog))
        block.gpsimd(run(self.engs["pool"].prog))
        block.sync(run(self.engs["sp"].prog))


VEC_SPEC = [("g_mix", 4 * 16), ("g_ffn", 4 * 16), ("conv_w", 2 * 4 * 16), ("conv_b", 2 * 16),
            ("b_a", 2 * 16), ("b_i", 2 * 16), ("lam", 2 * 16), ("g_kv", 16), ("g_final", 16),
            ("ada_b", 4 * 6 * 16)]
VEC_OFF = {}
_o = 0
for _n, _s in VEC_SPEC:
    VEC_OFF[_n] = _o
    _o += _s
NV = _o


def build_nc():
    nc = bass.Bass("TRN2", target_bir_lowering=False)

    def din(name, shape, dt=F32):
        return nc.dram_tensor(name, list(shape), dt, kind="ExternalInput").ap()

    def dout(name, shape):
        return nc.dram_tensor(name, list(shape), F32, kind="ExternalOutput").ap()

    xT_p = din("xT_p", [8, 128, 8192])
    xT_s = din("xT_s", [128, 16, 256])
    cT_d = din("cT", [128, 96])
    vecs_d = din("vecs", [128, NV])
    conv_in_d = din("conv_in", [128, 384])
    rnn_in_d = din("rnn_in", [128, 128])
    kTc_d = din("kTc", [4, 128, 8192])
    vc_d = din("vc", [4, 128, 8192])
    dtab_d = din("dtab", [2, 128, 4096])
    cfar_d = din("cfar", [128, 32])
    w_ada = din("w_ada", [4 * 96, 128, 2048])
    w_in = din("w_in", [32, 128, 2048])
    w_gate = din("w_gate", [32, 128, 2048])
    w_ai = din("w_ai", [16, 128, 1024])
    w_out = din("w_out", [32, 128, 2048])
    w_k = din("w_k", [16, 128, 2048])
    w_v = din("w_v", [16, 128, 2048])
    w_q = din("w_q", [32, 128, 2048])
    w_o = din("w_o", [32, 128, 2048])
    w_1 = din("w_1", [4 * 44, 128, 2048])
    w_3 = din("w_3", [4 * 44, 128, 2048])
    w_2 = din("w_2", [4 * 4 * 16, 128, 1408])

    yT_p = dout("yT_p", [8, 128, 16, 512])
    yT_s = dout("yT_s", [128, 16, 256])
    conv_p = dout("conv_p", [2, 128, 96])
    rnn_p = dout("rnn_p", [2, 128, 32])
    conv_s = dout("conv_s", [128, 384])
    rnn_s = dout("rnn_s", [128, 128])
    kT_p = dout("kT_p", [2, 128, 16, 512])
    v_p = dout("v_p", [2, 512, 2048])
    kT_s = dout("kT_s", [128, 16, 256])
    v_s = dout("v_s", [4, 64, 2048])

    with contextlib.ExitStack() as st:
        S = Sched(nc, st)

        def sb(name, shape, dt):
            return st.enter_context(nc.sbuf_tensor("sb_" + name, list(shape), dt))

        x = sb("x", [128, 16, 512], F32)
        hn = sb("hn", [128, 16, 512], BF16)
        mix = sb("mix", [128, 16, 512], BF16)
        bandK = [sb("bK%d" % i, [128, 16, 512], BF16) for i in range(2)]
        bandV = [sb("bV%d" % i, [128, 4, 2048], BF16) for i in range(2)]
        wsl = [sb("wsl%d" % i, [128, 2048], BF16) for i in range(NW)]
        scr = [sb("scr%d" % i, [128, 520], F32) for i in range(NS)]
        expD = sb("expD", [128, 16, 2, 128], BF16)
        modt = sb("modt", [128, 4, 6, 16, 6], F32)
        vecs = sb("vecs", [128, NV], F32)
        kc8 = sb("kc8", [128, 2, 2, 16], F32)
        hb = sb("hb", [128, 2, 32], F32)
        cfar = sb("cfar", [128, 32], F32)
        cTf = sb("cTf", [128, 96], F32)
        cTb = sb("cTb", [128, 16, 6], BF16)
        cst_p = sb("cst_p", [128, 2, 16, 1, 3], F32)
        hst_p = sb("hst_p", [128, 2, 16, 1], F32)
        cst_s = sb("cst_s", [128, 2, 16, 4, 3], F32)
        hst_s = sb("hst_s", [128, 2, 16, 4], F32)
        conv_in = sb("conv_in", [128, 2, 16, 4, 3], F32)
        rnn_in = sb("rnn_in", [128, 2, 16, 4], F32)
        ones_b = sb("ones_b", [128, 128], BF16)
        onesm_b = sb("onesm_b", [128, 128], BF16)
        sptmp = sb("sptmp", [128, 6, 32], F32)
        ps = [st.enter_context(nc.psum_tensor("ps%d" % i, [128, 512], F32)) for i in range(8)]

        Rx = [Res("x%d" % c) for c in range(16)]
        Rhn = [Res("hn%d" % c) for c in range(16)]
        Rmix = [Res("mix%d" % c) for c in range(16)]
        RbK = [[Res("bK%d_%d" % (i, h)) for h in range(16)] for i in range(2)]
        RbV = [[Res("bV%d_%d" % (i, b)) for b in range(4)] for i in range(2)]
        Rw = [Res("w%d" % i) for i in range(NW)]
        Cw = [S.chan() for _ in range(NW)]
        Rs = [Res("scr%d" % i) for i in range(NS)]
        P = [Res("ps%d" % i, excl=True) for i in range(8)]
        RexpD = Res("expD")
        Rmod = Res("mod")
        Rvec = Res("vecs")
        Rkc8 = Res("kc8")
        Rcfar = Res("cfar")
        RcT = Res("cT")
        Rcstp, Rhstp, Rcsts, Rhsts = Res("cstp"), Res("hstp"), Res("csts"), Res("hsts")
        Rcin, Rrin = Res("cin"), Res("rin")
        Rones = Res("ones")
        Rsp = Res("sptmp")
        c_in = S.chan()
        c_x = S.chan()
        c_out = [S.chan() for _ in range(4)]
        c_bk = [S.chan() for _ in range(2)]
        c_bv = [S.chan() for _ in range(2)]

        role_rr = {}

        def bank(role):
            base = {"A": 0, "B": 2, "C": 4, "D": 6}[role]
            k = role_rr.get(role, 0)
            role_rr[role] = k ^ 1
            return base + k

        w_rr = [0]

        def wload(dram_ap, n):
            i = w_rr[0]
            w_rr[0] = (i + 1) % NW
            S.op("pool", lambda E: E.dma_start(out=wsl[i][:, 0:n], in_=dram_ap), writes=[Rw[i]], chan=Cw[i])
            return wsl[i], Rw[i]

        def mmg(out_ap, pairs, reads, pres):
            pairs = list(pairs)

            def fn(E):
                n = len(pairs)
                ins = None
                for k, (l, r) in enumerate(pairs):
                    ins = E.matmul(out_ap, l, r, start=(k == 0), stop=(k == n - 1))
                return ins
            S.op("pe", fn, reads=reads, writes=[pres])

        def mm_kmajor(items, src, Rsrc, nk):
            for k in range(nk):
                def fn(E, k=k):
                    ins = None
                    for (oap, wv, wres, pres) in items:
                        ins = E.matmul(oap, wv[:, k, :], src(k), start=(k == 0), stop=(k == nk - 1))
                    return ins
                S.op("pe", fn, reads=[Rsrc[k]] + [it[2] for it in items], writes=[it[3] for it in items])

        def V_(name, idx):
            o = VEC_OFF[name] + idx
            return vecs[:, o:o + 1]

        S.op("sp", lambda E: E.dma_start(out=vecs[:], in_=vecs_d), writes=[Rvec], chan=c_in)
        S.op("sp", lambda E: E.dma_start(out=cTf[:], in_=cT_d), writes=[RcT], chan=c_in)
        S.op("sp", lambda E: E.dma_start(out=cfar[:], in_=cfar_d), writes=[Rcfar], chan=c_in)
        S.op("sp", lambda E: E.dma_start(out=conv_in[:].rearrange("p a b c d -> p (a b c d)"), in_=conv_in_d), writes=[Rcin], chan=c_in)
        S.op("sp", lambda E: E.dma_start(out=rnn_in[:].rearrange("p a b c -> p (a b c)"), in_=rnn_in_d), writes=[Rrin], chan=c_in)
        for R_ in (Rvec, RcT, Rcfar, Rcin, Rrin):
            R_.w = (c_in, c_in.cnt)
        S.op("dve", lambda E: E.memset(ones_b[:], 1.0), writes=[Rones])
        S.op("dve", lambda E: E.memset(onesm_b[:], 1.0 / D), writes=[Rones])
        S.op("act", lambda E: E.activation(cTb[:].rearrange("p a b -> p (a b)"), cTf[:], AF.Silu), reads=[RcT], writes=[RcT])

        lamv = vecs[:, VEC_OFF["lam"]:VEC_OFF["lam"] + 32]
        t_abs, t_e, t_z, t_w, t_p, t_m = (sptmp[:, k, :] for k in range(6))
        S.op("act", lambda E: E.activation(t_abs, lamv, AF.Abs), reads=[Rvec], writes=[Rsp])
        S.op("act", lambda E: E.activation(t_e, t_abs, AF.Exp, scale=-1.0), reads=[Rsp], writes=[Rsp])
        S.op("dve", lambda E: E.tensor_scalar(t_w, t_e, 2.0, None, ALU.add), reads=[Rsp], writes=[Rsp])
        S.op("dve", lambda E: E.reciprocal(t_w, t_w), reads=[Rsp], writes=[Rsp])
        S.op("dve", lambda E: E.tensor_tensor(t_z, t_e, t_w, ALU.mult), reads=[Rsp], writes=[Rsp])
        S.op("dve", lambda E: E.tensor_tensor(t_w, t_z, t_z, ALU.mult), reads=[Rsp], writes=[Rsp])
        S.op("dve", lambda E: E.memset(t_p, 1.0 / 17), writes=[Rsp])
        for kk in (15, 13, 11, 9, 7, 5, 3, 1):
            S.op("dve", lambda E: E.tensor_tensor(t_p, t_p, t_w, ALU.mult), reads=[Rsp], writes=[Rsp])
            S.op("dve", lambda E, kk=kk: E.tensor_scalar(t_p, t_p, 1.0 / kk, None, ALU.add), reads=[Rsp], writes=[Rsp])
        S.op("dve", lambda E: E.tensor_tensor(t_p, t_p, t_z, ALU.mult), reads=[Rsp], writes=[Rsp])
        S.op("dve", lambda E: E.tensor_scalar(t_m, lamv, -1.0, 0.0, ALU.mult, ALU.max), reads=[Rvec, Rsp], writes=[Rsp])
        S.op("dve", lambda E: E.scalar_tensor_tensor(t_p, t_p, 2.0, t_m, ALU.mult, ALU.add), reads=[Rsp], writes=[Rsp])
        S.op("dve", lambda E: E.tensor_scalar(kc8[:, 0].rearrange("p a b -> p (a b)"), t_p, -4.0, None, ALU.mult), reads=[Rsp], writes=[Rkc8])
        S.op("dve", lambda E: E.tensor_scalar(kc8[:, 1].rearrange("p a b -> p (a b)"), t_p, -8.0, None, ALU.mult), reads=[Rsp], writes=[Rkc8])
        S.op("dve", lambda E: E.tensor_scalar(hb[:, 0, :], vecs[:, VEC_OFF["b_a"]:VEC_OFF["b_a"] + 32], 0.5, None, ALU.mult), reads=[Rvec], writes=[Rkc8])
        S.op("dve", lambda E: E.tensor_scalar(hb[:, 1, :], vecs[:, VEC_OFF["b_i"]:VEC_OFF["b_i"] + 32], 0.5, None, ALU.mult), reads=[Rvec], writes=[Rkc8])

        for L in range(4):
            for j in range(6):
                b = bank("A")
                for n in range(16):
                    wt, wr = wload(w_ada[L * 96 + j * 16 + n], 2048)
                    wv = wt[:].rearrange("p (k n) -> p k n", k=16)
                    mmg(ps[b][:, n * 6:(n + 1) * 6], [(wv[:, k, :], cTb[:, k, :]) for k in range(16)], [wr, RcT], P[b])
                ob = VEC_OFF["ada_b"] + (L * 6 + j) * 16
                for n in range(16):
                    S.op("dve", lambda E, b=b, L=L, j=j, n=n, ob=ob: E.tensor_scalar(
                        modt[:, L, j, n, :], ps[b][:, n * 6:(n + 1) * 6], vecs[:, ob + n:ob + n + 1], None, ALU.add),
                        reads=[P[b], Rvec], writes=[Rmod])
        for L in range(4):
            for (j, gname) in ((1, "g_mix"), (4, "g_ffn")):
                og = VEC_OFF[gname] + L * 16
                for n in range(16):
                    S.op("dve", lambda E, L=L, j=j, n=n, og=og: E.tensor_scalar(
                        modt[:, L, j, n, :], modt[:, L, j, n, :], 1.0, vecs[:, og + n:og + n + 1], ALU.add, ALU.mult),
                        reads=[Rmod, Rvec], writes=[Rmod])

        def M_(L, j, c, s):
            return modt[:, L, j, c, s:s + 1]

        class Tile:
            pass

        def norm_stats(tc, xs, Rxs):
            T = tc.T
            b = getattr(tc, "stats_bank", None)
            tc.stats_bank = None
            if b is None:
                b = bank("D")
                for c in range(16):
                    si = 8 + (c % 2)
                    sqv = scr[si][:].bitcast(BF16)[:, 0:T]
                    if c % 2 == 0:
                        S.op("act", lambda E, sqv=sqv, c=c: E.activation(sqv, xs(c), AF.Square), reads=[Rxs[c]], writes=[Rs[si]])
                    else:
                        S.op("dve", lambda E, sqv=sqv, c=c: E.tensor_tensor(sqv, xs(c), xs(c), ALU.mult), reads=[Rxs[c]], writes=[Rs[si]])

                    def fn(E, sqv=sqv, c=c, b=b):
                        return E.matmul(ps[b][:, 0:T], onesm_b[:], sqv, start=(c == 0), stop=(c == 15))
                    S.op("pe", fn, reads=[Rs[si], Rones], writes=[P[b]])
            rs = scr[10][:, 0:T]
            S.op("act", lambda E: E.activation(rs, ps[b][:, 0:T], AF.Sqrt, bias=EPS), reads=[P[b]], writes=[Rs[10]])
            S.op("dve", lambda E: E.reciprocal(rs, rs), reads=[Rs[10]], writes=[Rs[10]])
            return rs, Rs[10]

        def norm_apply(tc, xs, Rxs, rs, Rrs, scale_fn, bias_fn, dst, Rdst_):
            T = tc.T
            for c in range(16):
                si = 11 + (c % 2)
                tmp = scr[si][:, 0:T]
                S.op("dve", lambda E, tmp=tmp, c=c: E.tensor_tensor(tmp, xs(c), rs, ALU.mult), reads=[Rxs[c], Rrs], writes=[Rs[si]])
                for (c0, n, sidx) in tc.segs:
                    if bias_fn is None:
                        S.op("act", lambda E, tmp=tmp, c=c, c0=c0, n=n, sidx=sidx: E.activation(
                            dst(c)[:, c0:c0 + n], tmp[:, c0:c0 + n], AF.Identity, scale=scale_fn(c, sidx)),
                            reads=[Rs[si], Rvec, Rmod], writes=[Rdst_[c]])
                    else:
                        S.op("act", lambda E, tmp=tmp, c=c, c0=c0, n=n, sidx=sidx: E.activation(
                            dst(c)[:, c0:c0 + n], tmp[:, c0:c0 + n], AF.Identity, scale=scale_fn(c, sidx), bias=bias_fn(c, sidx)),
                            reads=[Rs[si], Rvec, Rmod], writes=[Rdst_[c]])

        def proj_resid(tc, wd, wbase, src, Rsrc, nk, wn, gate_j, L, xs, Rxs, stats=False):
            T = tc.T
            bD = bank("D") if stats else None
            pend = []

            def flush():
                for (n_, sqv, si) in pend:
                    def fn(E, sqv=sqv, n_=n_):
                        return E.matmul(ps[bD][:, 0:T], onesm_b[:], sqv, start=(n_ == 0), stop=(n_ == 15))
                    S.op("pe", fn, reads=[Rs[si], Rones], writes=[P[bD]])
                del pend[:]
            for n in range(16):
                wt, wr = wload(wd[wbase + n], wn)
                wv = wt[:, 0:wn].rearrange("p (k n) -> p k n", k=nk)
                b = bank("C")
                mmg(ps[b][:, 0:T], [(wv[:, k, :], src(k)) for k in range(nk)], [wr] + [Rsrc[k] for k in range(nk)], P[b])
                flush()
                for (c0, nn, sidx) in tc.segs:
                    S.op("dve", lambda E, b=b, n=n, c0=c0, nn=nn, sidx=sidx: E.scalar_tensor_tensor(
                        xs(n)[:, c0:c0 + nn], ps[b][:, c0:c0 + nn], M_(L, gate_j, n, sidx), xs(n)[:, c0:c0 + nn],
                        ALU.mult, ALU.add), reads=[P[b], Rmod, Rxs[n]], writes=[Rxs[n]])
                if stats:
                    si = 8 + (n % 2)
                    sqv = scr[si][:].bitcast(BF16)[:, 0:T]
                    S.op("act", lambda E, sqv=sqv, n=n: E.activation(sqv, xs(n), AF.Square), reads=[Rxs[n]], writes=[Rs[si]])
                    pend.append((n, sqv, si))
            flush()
            if stats:
                tc.stats_bank = bD

        def ffn(tc, L, xs, Rxs, hns, Rhns, mixs, Rmixs):
            T = tc.T
            rs, Rrs = norm_stats(tc, xs, Rxs)
            norm_apply(tc, xs, Rxs, rs, Rrs, lambda c, s: M_(L, 4, c, s), lambda c, s: M_(L, 3, c, s), hns, Rhns)
            for g in range(NGRP):
                jj = 0
                while jj < GSZ:
                    pair = [jj, jj + 1] if (g == 0 and jj == 0) else [jj]
                    items, meta = [], []
                    for j2 in pair:
                        fc = g * GSZ + j2
                        w1t, w1r = wload(w_1[L * 44 + fc], 2048)
                        w3t, w3r = wload(w_3[L * 44 + fc], 2048)
                        w1v = w1t[:].rearrange("p (k n) -> p k n", k=16)
                        w3v = w3t[:].rearrange("p (k n) -> p k n", k=16)
                        ba, bb = bank("A"), bank("B")
                        items.append((ps[ba][:, 0:T], w1v, w1r, P[ba]))
                        items.append((ps[bb][:, 0:T], w3v, w3r, P[bb]))
                        meta.append((j2, fc, ba, bb))
                    if len(pair) == 2:
                        mm_kmajor(items, hns, Rhns, 16)
                    else:
                        for (oap, wv, wres, pres) in items:
                            mmg(oap, [(wv[:, k, :], hns(k)) for k in range(16)], [wres] + Rhns, pres)
                    for (j2, fc, ba, bb) in meta:
                        si = 13 + (fc % 2)
                        sa = scr[si][:, 0:T]
                        S.op("act", lambda E, sa=sa, ba=ba: E.activation(sa, ps[ba][:, 0:T], AF.Silu), reads=[P[ba]], writes=[Rs[si]])
                        S.op("dve", lambda E, sa=sa, bb=bb, j2=j2: E.tensor_tensor(mixs(j2), sa, ps[bb][:, 0:T], ALU.mult),
                             reads=[Rs[si], P[bb]], writes=[Rmixs[j2]])
                    jj += len(pair)
                proj_resid(tc, w_2, (L * 4 + g) * 16, mixs, Rmixs, GSZ, 1408, 5, L, xs, Rxs, stats=(g == NGRP - 1))

        def rg_layer(tc, L, xs, Rxs, hns, Rhns, mixs, Rmixs):
            T = tc.T
            nseg, SL = tc.nseg, tc.SL
            rs, Rrs = norm_stats(tc, xs, Rxs)
            norm_apply(tc, xs, Rxs, rs, Rrs, lambda c, s: M_(L, 1, c, s), lambda c, s: M_(L, 0, c, s), hns, Rhns)
            cst, Rcst = (cst_s, Rcsts) if tc.sample else (cst_p, Rcstp)
            hst, Rhst = (hst_s, Rhsts) if tc.sample else (hst_p, Rhstp)
            W3 = nseg * (SL + 3)

            def stage_a(blk):
                par = blk % 2
                pre0 = []
                if blk == 0:
                    items = []
                    for cc in range(2):
                        wit, wir = wload(w_in[L * 16 + cc], 2048)
                        wgt, wgr = wload(w_gate[L * 16 + cc], 2048)
                        ba, bb = bank("A"), bank("B")
                        items.append((ps[ba][:, 0:T], wit[:].rearrange("p (k n) -> p k n", k=16), wir, P[ba]))
                        items.append((ps[bb][:, 0:T], wgt[:].rearrange("p (k n) -> p k n", k=16), wgr, P[bb]))
                        pre0.append((ba, bb))
                    mm_kmajor(items, hns, Rhns, 16)
                for cc in range(2):
                    c = blk * 2 + cc
                    if blk != 0:
                        wit, wir = wload(w_in[L * 16 + c], 2048)
                        wgt, wgr = wload(w_gate[L * 16 + c], 2048)
                        wiv = wit[:].rearrange("p (k n) -> p k n", k=16)
                        wgv = wgt[:].rearrange("p (k n) -> p k n", k=16)
                    if blk == 0:
                        ba, bb = pre0[cc]
                    else:
                        ba, bb = bank("A"), bank("B")
                        mmg(ps[ba][:, 0:T], [(wiv[:, k, :], hns(k)) for k in range(16)], [wir] + Rhns, P[ba])
                        mmg(ps[bb][:, 0:T], [(wgv[:, k, :], hns(k)) for k in range(16)], [wgr] + Rhns, P[bb])
                    sxb, sxc, sg, sxcb = cc, 2 + 2 * par + cc, 6 + par, 8 + par
                    xb3 = scr[sxb][:, 0:W3].rearrange("p (s l) -> p s l", s=nseg)
                    xc = scr[sxc][:, 0:T]
                    xc3 = xc.rearrange("p (s l) -> p s l", s=nseg)
                    gb = scr[sg][:].bitcast(BF16)[:, cc * 512:cc * 512 + T]
                    xcb = scr[sxcb][:].bitcast(BF16)[:, cc * 512:cc * 512 + T]
                    pa3 = ps[ba][:, 0:T].rearrange("p (s l) -> p s l", s=nseg)
                    Rxb, Rxc, Rg, Rxcb = Rs[sxb], Rs[sxc], Rs[sg], Rs[sxcb]
                    if tc.sample:
                        S.op("dve", lambda E, xb3=xb3, c=c: E.tensor_copy(xb3[:, :, 0:3], conv_in[:, L, c]), reads=[Rcin], writes=[Rxb])
                    elif tc.j == 0:
                        S.op("dve", lambda E, xb3=xb3: E.memset(xb3[:, :, 0:3], 0.0), writes=[Rxb])
                    else:
                        S.op("dve", lambda E, xb3=xb3, c=c: E.tensor_copy(xb3[:, :, 0:3], cst[:, L, c]), reads=[Rcst], writes=[Rxb])
                    S.op("dve", lambda E, xb3=xb3, pa3=pa3: E.tensor_copy(xb3[:, :, 3:3 + SL], pa3), reads=[P[ba]], writes=[Rxb])
                    S.op("act", lambda E, xc=xc, ba=ba, c=c: E.activation(
                        xc, ps[ba][:, 0:T], AF.Identity, scale=V_("conv_w", (L * 4 + 3) * 16 + c), bias=V_("conv_b", L * 16 + c)),
                        reads=[P[ba], Rvec], writes=[Rxc])
                    for k in range(3):
                        S.op("dve", lambda E, xc3=xc3, xb3=xb3, k=k, c=c: E.scalar_tensor_tensor(
                            xc3, xb3[:, :, k:k + SL], V_("conv_w", (L * 4 + k) * 16 + c), xc3, ALU.mult, ALU.add),
                            reads=[Rxb, Rvec, Rxc], writes=[Rxc])
                    S.op("dve", lambda E, xcb=xcb, xc=xc: E.tensor_copy(xcb, xc), reads=[Rxc], writes=[Rxcb])
                    S.op("dve", lambda E, xb3=xb3, c=c: E.tensor_copy(cst[:, L, c], xb3[:, :, SL:SL + 3]), reads=[Rxb], writes=[Rcst])
                    S.op("act", lambda E, gb=gb, bb=bb: E.activation(gb, ps[bb][:, 0:T], AF.Gelu_apprx_tanh), reads=[P[bb]], writes=[Rg])

            def stage_b(blk):
                par = blk % 2
                wai, wair = wload(w_ai[L * 8 + blk], 1024)
                waiv = wai[:, 0:1024].rearrange("p (g k n) -> p g k n", g=2, k=2)
                sg, sxcb = 6 + par, 8 + par
                xcbs = [scr[sxcb][:].bitcast(BF16)[:, k * 512:k * 512 + T] for k in range(2)]
                i_, h_ = scr[12][:, 0:T], scr[15][:, 0:T]
                Ri, Rh = Rs[12], Rs[15]
                for cc in range(2):
                    c = blk * 2 + cc
                    sxc = 2 + 2 * par + cc
                    xc = scr[sxc][:, 0:T]
                    Rxc = Rs[sxc]
                    r_, a_ = scr[10 + cc][:, 0:T], scr[13 + cc][:, 0:T]
                    Rr, Ra = Rs[10 + cc], Rs[13 + cc]
                    bc, bd = bank("C"), bank("D")
                    mmg(ps[bc][:, 0:T], [(waiv[:, 0, k, cc * 128:(cc + 1) * 128], xcbs[k]) for k in range(2)], [wair, Rs[sxcb]], P[bc])
                    mmg(ps[bd][:, 0:T], [(waiv[:, 1, k, cc * 128:(cc + 1) * 128], xcbs[k]) for k in range(2)], [wair, Rs[sxcb]], P[bd])
                    S.op("act", lambda E, r_=r_, bc=bc, c=c: E.activation(r_, ps[bc][:, 0:T], AF.Tanh, scale=0.5, bias=hb[:, 0, L * 16 + c:L * 16 + c + 1]), reads=[P[bc], Rkc8], writes=[Rr])
                    S.op("act", lambda E, bd=bd, c=c: E.activation(i_, ps[bd][:, 0:T], AF.Tanh, scale=0.5, bias=hb[:, 1, L * 16 + c:L * 16 + c + 1]), reads=[P[bd], Rkc8], writes=[Ri])
                    S.op("act", lambda E, a_=a_, r_=r_, c=c: E.activation(a_, r_, AF.Exp, scale=kc8[:, 0, L, c:c + 1], bias=kc8[:, 0, L, c:c + 1]), reads=[Rr, Rkc8], writes=[Ra])
                    S.op("act", lambda E, r_=r_, c=c: E.activation(r_, r_, AF.Exp, scale=kc8[:, 1, L, c:c + 1], bias=kc8[:, 1, L, c:c + 1]), reads=[Rr, Rkc8], writes=[Rr])
                    S.op("dve", lambda E, xc=xc: E.scalar_tensor_tensor(xc, i_, 1.0, xc, ALU.add, ALU.mult), reads=[Ri, Rxc], writes=[Rxc])
                for cc in range(2):
                    r_ = scr[10 + cc][:, 0:T]
                    Rr = Rs[10 + cc]
                    S.op("act", lambda E, r_=r_: E.activation(r_, r_, AF.Sqrt, scale=-0.25, bias=0.25), reads=[Rr], writes=[Rr])
                for cc in range(2):
                    c = blk * 2 + cc
                    sxc = 2 + 2 * par + cc
                    xc = scr[sxc][:, 0:T]
                    Rxc = Rs[sxc]
                    gb = scr[sg][:].bitcast(BF16)[:, cc * 512:cc * 512 + T]
                    Rg = Rs[sg]
                    r_, a_ = scr[10 + cc][:, 0:T], scr[13 + cc][:, 0:T]
                    Rr, Ra = Rs[10 + cc], Rs[13 + cc]
                    if (not tc.sample) and tc.j == 0:
                        S.op("dve", lambda E, r_=r_: E.memset(r_[:, 0:1], 0.5), reads=[Rr], writes=[Rr])
                    S.op("dve", lambda E, xc=xc, r_=r_: E.tensor_tensor(xc, xc, r_, ALU.mult), reads=[Rxc, Rr], writes=[Rxc])
                    for s_ in range(nseg):
                        if tc.sample:
                            init = rnn_in[:, L, c, s_:s_ + 1]
                            rd = [Rrin]
                        elif tc.j == 0:
                            init = 0.0
                            rd = []
                        else:
                            init = hst[:, L, c, 0:1]
                            rd = [Rhst]
                        S.op("dve", lambda E, a_=a_, xc=xc, s_=s_, init=init: E.tensor_tensor_scan(
                            h_[:, s_ * SL:(s_ + 1) * SL], a_[:, s_ * SL:(s_ + 1) * SL], xc[:, s_ * SL:(s_ + 1) * SL], init, ALU.mult, ALU.add),
                            reads=[Ra, Rxc] + rd, writes=[Rh])
                    h3 = h_.rearrange("p (s l) -> p s l", s=nseg)
                    S.op("dve", lambda E, h3=h3, c=c: E.tensor_copy(hst[:, L, c], h3[:, :, SL - 1]), reads=[Rh], writes=[Rhst])
                    S.op("dve", lambda E, gb=gb, c=c: E.tensor_tensor(mixs(c), h_, gb, ALU.mult), reads=[Rh, Rg], writes=[Rmixs[c]])

            stage_a(0)
            for blk in range(1, 8):
                stage_a(blk)
                stage_b(blk - 1)
            stage_b(7)
            proj_resid(tc, w_out, L * 16, mixs, Rmixs, 16, 2048, 2, L, xs, Rxs, stats=True)

        def load_expD(bl):
            for q4 in range(4):
                i = w_rr[0]
                w_rr[0] = (i + 1) % NW
                wf = wsl[i][:].bitcast(F32)
                S.op("sp", lambda E, wf=wf, q4=q4: E.dma_start(out=wf, in_=dtab_d[bl, :, q4 * 1024:(q4 + 1) * 1024]), writes=[Rw[i]], chan=Cw[i])
                S.op("act", lambda E, wf=wf, q4=q4: E.activation(
                    expD[:, q4 * 4:(q4 + 1) * 4].rearrange("p h t q -> p (h t q)"), wf, AF.Exp), reads=[Rw[i]], writes=[RexpD])
            S.op("dve", lambda E: E.memset(expD[64:128, :, 1, 0:64], 0.0), reads=[RexpD], writes=[RexpD])

        def attn_run(groups, bl):
            ng = len(groups)
            state = {}

            def qk_exp(k):
                g = groups[k]
                st_ = k % 3
                b0, b1 = 2 * st_, 2 * st_ + 1
                nq = g["nq"]
                qap, qres = g["q"]
                h = g["h"]
                pt = scr[0 + st_][:].bitcast(BF16)
                ev = scr[3 + st_]
                Rpt, Rev = Rs[0 + st_], Rs[3 + st_]
                keys = g["keys"]
                far = [kx for kx in keys if kx[0] <= 2]
                nd = [kx for kx in keys if kx[0] >= 3]
                allk = []
                for kx in keys:
                    allk += kx[5]

                def sps_of(pos, nk):
                    if pos <= 2:
                        return ps[b0][0:nk, pos * 128:pos * 128 + nq]
                    return ps[b1][0:nk, (pos - 3) * 128:(pos - 3) * 128 + nq]

                def fn(E):
                    ins = None
                    for (pos, KT, Vv, nk, kind, kres) in keys:
                        ins = E.matmul(sps_of(pos, nk), KT, qap, start=True, stop=True)
                    return ins
                wr = ([P[b0]] if far else []) + ([P[b1]] if nd else [])
                S.op("pe", fn, reads=allk + qres, writes=wr)
                if far:
                    lo = far[0][0]
                    src = ps[b0][:, 0:384].rearrange("p (i q) -> p i q", q=128)[:, lo:3, 0:nq]
                    dst = pt[:, 0:384].rearrange("p (i q) -> p i q", q=128)[:, lo:3, 0:nq]
                    S.op("act", lambda E: E.activation(dst, src, AF.Exp, bias=cfar[:, bl * 16 + h:bl * 16 + h + 1]),
                         reads=[P[b0], Rcfar], writes=[Rpt])
                    if far[0][4] == "farmask":
                        S.op("dve", lambda E: E.memset(pt[0:64, 64:128], 0.0), reads=[Rpt], writes=[Rpt])
                if len(nd) == 2 and nd[0][3] == 128 and nd[1][3] == 128 and nq == 128:
                    S.op("act", lambda E: E.activation(ev[:, 0:256], ps[b1][:, 0:256], AF.Exp), reads=[P[b1]], writes=[Rev])
                    S.op("dve", lambda E: E.tensor_tensor(pt[:, 384:640], ev[:, 0:256], expD[:, h].rearrange("p t q -> p (t q)"), ALU.mult),
                         reads=[Rev, RexpD], writes=[Rpt])
                else:
                    for (pos, KT, Vv, nk, kind, kres) in nd:
                        e = ev[0:nk, (pos - 3) * 128:(pos - 3) * 128 + nq]
                        sps = sps_of(pos, nk)
                        pti = pt[0:nk, pos * 128:pos * 128 + nq]
                        S.op("act", lambda E, e=e, sps=sps: E.activation(e, sps, AF.Exp), reads=[P[b1]], writes=[Rev])
                        S.op("dve", lambda E, pti=pti, e=e, nk=nk, pos=pos: E.tensor_tensor(pti, e, expD[0:nk, h, pos - 3, 0:nq], ALU.mult),
                             reads=[Rev, RexpD], writes=[Rpt])
                ptiles = [(pt[0:nk, pos * 128:pos * 128 + nq], Vv, nk) for (pos, KT, Vv, nk, kind, kres) in keys]
                state[k] = (ptiles, allk, b1, st_)

            def pv_norm(k):
                g = groups[k]
                ptiles, allk, b1, st_ = state.pop(k)
                nq = g["nq"]
                Rpt, Rev = Rs[0 + st_], Rs[3 + st_]
                ev = scr[3 + st_]
                ops_ = ps[b1][:, 384:384 + nq]
                dps = ps[b1][:, 256:256 + nq]
                mmg(ops_, [(Vv, pti) for (pti, Vv, nk) in ptiles], [Rpt] + allk, P[b1])
                mmg(dps, [(ones_b[0:nk, :], pti) for (pti, Vv, nk) in ptiles], [Rpt, Rones], P[b1])
                rd = ev[:, 256:256 + nq]
                S.op("dve", lambda E: E.reciprocal(rd, dps), reads=[P[b1]], writes=[Rev])
                oap, ores = g["out"]
                S.op("dve", lambda E: E.tensor_tensor(oap, ops_, rd, ALU.mult), reads=[P[b1], Rev], writes=[ores])

            for k in range(ng + 1):
                if k < ng:
                    if groups[k].get("pre") is not None:
                        groups[k]["pre"]()
                    qk_exp(k)
                if k >= 1:
                    pv_norm(k - 1)

        def q_proj(tc, bl, h, hns, Rhns, dst_ap, dst_res):
            T = tc.T
            wt, wr = wload(w_q[bl * 16 + h], 2048)
            wv = wt[:].rearrange("p (k n) -> p k n", k=16)
            b = bank("D")
            mmg(ps[b][:, 0:T], [(wv[:, k, :], hns(k)) for k in range(16)], [wr] + Rhns, P[b])
            S.op("act", lambda E: E.activation(dst_ap, ps[b][:, 0:T], AF.Copy, scale=QSCALE), reads=[P[b]], writes=[dst_res])

        def kv_proj(tc, hns, Rhns, KT_dst, RKT, vblocks, V_dst_fn, k_out_fn):
            T = tc.T
            kb0 = []
            items = []
            for h in range(4):
                wt, wr = wload(w_k[h], 2048)
                b = bank("A") if h < 2 else bank("B")
                items.append((ps[b][:, 0:T], wt[:].rearrange("p (k n) -> p k n", k=16), wr, P[b]))
                kb0.append(b)
            mm_kmajor(items, hns, Rhns, 16)
            for h in range(16):
                if h < 4:
                    b = kb0[h]
                else:
                    wt, wr = wload(w_k[h], 2048)
                    wv = wt[:].rearrange("p (k n) -> p k n", k=16)
                    b = bank("A")
                    mmg(ps[b][:, 0:T], [(wv[:, k, :], hns(k)) for k in range(16)], [wr] + Rhns, P[b])
                S.op("act", lambda E, h=h, b=b: E.activation(KT_dst(h), ps[b][:, 0:T], AF.Copy), reads=[P[b]], writes=[RKT[h]])
                if k_out_fn is not None:
                    si = 13 + (h % 2)
                    stg = scr[si][:, 0:T]
                    S.op("dve", lambda E, stg=stg, b=b: E.tensor_copy(stg, ps[b][:, 0:T]), reads=[P[b]], writes=[Rs[si]])
                    S.op("sp", lambda E, stg=stg, h=h: E.dma_start(out=k_out_fn(h), in_=stg), reads=[Rs[si]], chan=c_out[si - 13])
            for n4 in range(4):
                vbanks = [bank("A"), bank("A"), bank("B"), bank("B")]
                for kq in range(4):
                    wt, wr = wload(w_v[n4 * 4 + kq], 2048)
                    wv = wt[:].rearrange("p (k n) -> p k n", k=4)
                    for bi, (tb, ntok) in enumerate(vblocks):
                        b = vbanks[bi]

                        def fn(E, b=b, tb=tb, ntok=ntok, kq=kq, wv=wv):
                            ins = None
                            for kl in range(4):
                                ins = E.matmul(ps[b][0:ntok, 0:512], hns(kq * 4 + kl)[:, tb:tb + ntok], wv[:, kl, :],
                                               start=(kq == 0 and kl == 0), stop=(kq == 3 and kl == 3))
                            return ins
                        S.op("pe", fn, reads=[wr] + [Rhns[kq * 4 + kl] for kl in range(4)], writes=[P[b]])
                for bi, (tb, ntok) in enumerate(vblocks):
                    b = vbanks[bi]
                    pieces, vout = V_dst_fn(n4, bi)
                    for (dap, dres, c0, nc_) in pieces:
                        S.op("act", lambda E, b=b, ntok=ntok, dap=dap, c0=c0, nc_=nc_: E.activation(dap, ps[b][0:ntok, c0:c0 + nc_], AF.Copy),
                             reads=[P[b]], writes=[dres])
                    if vout is not None:
                        si = 13 + (bi % 2)
                        stg = scr[si][0:ntok, 0:512]
                        S.op("dve", lambda E, stg=stg, b=b, ntok=ntok: E.tensor_copy(stg, ps[b][0:ntok, 0:512]), reads=[P[b]], writes=[Rs[si]])
                        S.op("sp", lambda E, stg=stg, vout=vout: E.dma_start(out=vout, in_=stg), reads=[Rs[si]], chan=c_out[si - 13])

        def final_norm(tc, xs, Rxs, y_out_fn, after_chunk=None):
            T = tc.T
            rs, Rrs = norm_stats(tc, xs, Rxs)
            for c in range(16):
                si = 11 + (c % 2)
                tmp = scr[si][:, 0:T]
                S.op("dve", lambda E, tmp=tmp, c=c: E.tensor_tensor(tmp, xs(c), rs, ALU.mult), reads=[Rxs[c], Rrs], writes=[Rs[si]])
                so = 13 + (c % 2)
                stg = scr[so][:, 0:T]
                S.op("act", lambda E, tmp=tmp, stg=stg, c=c: E.activation(stg, tmp, AF.Identity, scale=V_("g_final", c)), reads=[Rs[si], Rvec], writes=[Rs[so]])
                S.op("sp", lambda E, stg=stg, c=c: E.dma_start(out=y_out_fn(c), in_=stg), reads=[Rs[so]], chan=c_out[so - 13])
                if after_chunk is not None:
                    after_chunk(c)

        def state_out(conv_dst, rnn_dst, cst, Rcst, hst, Rhst):
            S.op("sp", lambda E: E.dma_start(out=conv_dst, in_=cst[:].rearrange("p a b c d -> p (a b c d)")), reads=[Rcst], chan=c_out[2])
            S.op("sp", lambda E: E.dma_start(out=rnn_dst, in_=hst[:].rearrange("p a b c -> p (a b c)")), reads=[Rhst], chan=c_out[3])

        c_xs = [S.chan() for _ in range(16)]

        def x_load(g8_, c_):
            S.op("sp", lambda E: E.dma_start(out=x[:, c_, :], in_=xT_p[g8_][:, c_ * 512:(c_ + 1) * 512]), writes=[Rx[c_]], chan=c_xs[c_])

        for g8 in range(8):
            seq, j = g8 // 4, g8 % 4
            tc = Tile()
            tc.T, tc.segs, tc.j, tc.sample, tc.nseg, tc.SL = 512, [(0, 512, seq)], j, False, 1, 512
            xs = lambda c: x[:, c, :]
            hns = lambda c: hn[:, c, :]
            mixs = lambda c: mix[:, c, :]
            cur, prv = g8 % 2, (g8 + 1) % 2
            if g8 == 0:
                for c_ in range(16):
                    x_load(0, c_)
            for L in range(2):
                rg_layer(tc, L, xs, Rx, hns, Rhn, mixs, Rmix)
                if L == 1:
                    load_expD(0)
                ffn(tc, L, xs, Rx, hns, Rhn, mixs, Rmix)
            last = (j == 3)
            if last:
                state_out(conv_p[seq], rnn_p[seq], cst_p, Rcstp, hst_p, Rhstp)
            rs, Rrs = norm_stats(tc, xs, Rx)

            def V_dst_fn(n4, bi, cur=cur, seq=seq, last=last):
                return ([(bandV[cur][:, bi, n4 * 512:(n4 + 1) * 512], RbV[cur][bi], 0, 512)],
                        (v_p[seq, bi * 128:(bi + 1) * 128, n4 * 512:(n4 + 1) * 512] if last else None))
            norm_apply(tc, xs, Rx, rs, Rrs, lambda c, s: V_("g_kv", c), None, mixs, Rmix)
            norm_apply(tc, xs, Rx, rs, Rrs, lambda c, s: M_(2, 1, c, s), lambda c, s: M_(2, 0, c, s), hns, Rhn)
            kv_proj(tc, mixs, Rmix, lambda h, cur=cur: bandK[cur][:, h, :], RbK[cur], [(tb * 128, 128) for tb in range(4)], V_dst_fn,
                    (lambda h, seq=seq: kT_p[seq, :, h, :]) if last else None)
            for bl in range(2):
                L = 2 + bl
                if bl == 1:
                    rs, Rrs = norm_stats(tc, xs, Rx)
                    norm_apply(tc, xs, Rx, rs, Rrs, lambda c, s, L=L: M_(L, 1, c, s), lambda c, s, L=L: M_(L, 0, c, s), hns, Rhn)
                groups = []
                for h in range(16):
                    for qg in range(4):
                        keys = []
                        for i in range(5):
                            kb = qg - 4 + i
                            if kb < 0:
                                if j == 0:
                                    continue
                                buf, blk = prv, kb + 4
                            else:
                                buf, blk = cur, kb
                            kind = ("farmask", "far", "far", "near", "diag")[i]
                            keys.append((i, bandK[buf][:, h, blk * 128:(blk + 1) * 128], bandV[buf][:, blk, h * 128:(h + 1) * 128],
                                         128, kind, [RbK[buf][h], RbV[buf][blk]]))
                        qslot = 6 + (h % 2)
                        qap = scr[qslot][:].bitcast(BF16)[:, qg * 128:(qg + 1) * 128]
                        gd = dict(q=(qap, [Rs[qslot]]), nq=128, keys=keys, out=(mix[:, h, qg * 128:(qg + 1) * 128], Rmix[h]), h=h)
                        if qg == 1 and 1 <= h and h + 1 < 16:
                            gd["pre"] = (lambda h=h, bl=bl, tc=tc, hns=hns: q_proj(
                                tc, bl, h + 1, hns, Rhn, scr[6 + ((h + 1) % 2)][:].bitcast(BF16)[:, 0:512], Rs[6 + ((h + 1) % 2)]))
                        groups.append(gd)
                items, qb = [], []
                for h in range(2):
                    wt, wr = wload(w_q[bl * 16 + h], 2048)
                    b = bank("D")
                    items.append((ps[b][:, 0:512], wt[:].rearrange("p (k n) -> p k n", k=16), wr, P[b]))
                    qb.append(b)
                mm_kmajor(items, hns, Rhn, 16)
                for h in range(2):
                    S.op("act", lambda E, h=h, b=qb[h]: E.activation(scr[6 + h][:].bitcast(BF16)[:, 0:512], ps[b][:, 0:512], AF.Copy, scale=QSCALE),
                         reads=[P[qb[h]]], writes=[Rs[6 + h]])
                attn_run(groups, bl)
                proj_resid(tc, w_o, bl * 16, mixs, Rmix, 16, 2048, 2, L, xs, Rx, stats=True)
                if bl == 0:
                    load_expD(1)
                ffn(tc, L, xs, Rx, hns, Rhn, mixs, Rmix)
            final_norm(tc, xs, Rx, lambda c, g8=g8: yT_p[g8, :, c, :],
                       after_chunk=(lambda c, g8=g8: x_load(g8 + 1, c)) if g8 < 7 else None)

        S.barrier()
        tc = Tile()
        tc.T, tc.segs, tc.j, tc.sample, tc.nseg, tc.SL = 256, [(64 * i, 64, 2 + i) for i in range(4)], None, True, 4, 64
        xs = lambda c: x[:, c, 0:256]
        hns = lambda c: hn[:, c, 0:256]
        mixs = lambda c: mix[:, c, 0:256]
        Rx2 = [Res("x2_%d" % c) for c in range(16)]
        Rhn2 = [Res("hn2_%d" % c) for c in range(16)]
        Rmix2 = [Res("mix2_%d" % c) for c in range(16)]
        RKn = [Res("Kn%d" % h) for h in range(16)]
        RVn = [Res("Vn%d" % h) for h in range(16)]
        Rq = [Res("q%d" % h) for h in range(16)]
        RbK2 = [Res("bK2_%d" % i) for i in range(2)]
        RbV2 = [Res("bV2_%d" % i) for i in range(2)]
        KTn = lambda h: hn[:, h, 256:512]
        Vn = lambda h: x[0:64, h, 256:512].bitcast(BF16).rearrange("p (s d) -> p s d", s=4)
        qall = lambda h: mix[:, h, 256:512]
        S.op("sp", lambda E: E.dma_start(out=x[:, :, 0:256], in_=xT_s), writes=Rx2, chan=c_x)
        for L in range(2):
            rg_layer(tc, L, xs, Rx2, hns, Rhn2, mixs, Rmix2)
            ffn(tc, L, xs, Rx2, hns, Rhn2, mixs, Rmix2)
        state_out(conv_s, rnn_s, cst_s, Rcsts, hst_s, Rhsts)
        rs, Rrs = norm_stats(tc, xs, Rx2)

        def V_dst_fn_s(n4, bi):
            return ([(Vn(4 * n4 + hh)[:, bi, :], RVn[4 * n4 + hh], hh * 128, 128) for hh in range(4)],
                    v_s[bi, :, n4 * 512:(n4 + 1) * 512])
        norm_apply(tc, xs, Rx2, rs, Rrs, lambda c, s: V_("g_kv", c), None, mixs, Rmix2)
        norm_apply(tc, xs, Rx2, rs, Rrs, lambda c, s: M_(2, 1, c, s), lambda c, s: M_(2, 0, c, s), hns, Rhn2)
        kv_proj(tc, mixs, Rmix2, KTn, RKn, [(s_ * 64, 64) for s_ in range(4)], V_dst_fn_s, lambda h: kT_s[:, h, :])
        for bl in range(2):
            L = 2 + bl
            if bl == 1:
                rs, Rrs = norm_stats(tc, xs, Rx2)
                norm_apply(tc, xs, Rx2, rs, Rrs, lambda c, s, L=L: M_(L, 1, c, s), lambda c, s, L=L: M_(L, 0, c, s), hns, Rhn2)
            load_expD(bl)
            for h in range(16):
                q_proj(tc, bl, h, hns, Rhn2, qall(h), Rq[h])

            def load_cache(s):
                buf = s % 2
                S.op("pool", lambda E: E.dma_start(out=bandK[buf][:].rearrange("p h t -> p (h t)"), in_=kTc_d[s]), writes=[RbK2[buf]], chan=c_bk[buf])
                S.op("pool", lambda E: E.dma_start(out=bandV[buf][:].rearrange("p b f -> p (b f)"), in_=vc_d[s]), writes=[RbV2[buf]], chan=c_bv[buf])
            load_cache(0)
            load_cache(1)
            groups = []
            for s in range(4):
                buf = s % 2
                for h in range(16):
                    keys = []
                    for i in range(4):
                        kind = ("far", "far", "far", "near")[i]
                        keys.append((i, bandK[buf][:, h, i * 128:(i + 1) * 128], bandV[buf][:, i, h * 128:(h + 1) * 128], 128, kind,
                                     [RbK2[buf], RbV2[buf]]))
                    keys.append((4, KTn(h)[:, s * 64:(s + 1) * 64], Vn(h)[:, s, :], 64, "diag", [RKn[h], RVn[h]]))
                    gd = dict(q=(qall(h)[:, s * 64:(s + 1) * 64], [Rq[h]]), nq=64, keys=keys,
                              out=(mix[:, h, s * 64:(s + 1) * 64], Rmix2[h]), h=h)
                    if h == 2 and s >= 1 and s + 1 < 4:
                        gd["pre"] = (lambda s=s: load_cache(s + 1))
                    groups.append(gd)
            attn_run(groups, bl)
            proj_resid(tc, w_o, bl * 16, mixs, Rmix2, 16, 2048, 2, L, xs, Rx2, stats=True)
            ffn(tc, L, xs, Rx2, hns, Rhn2, mixs, Rmix2)
        final_norm(tc, xs, Rx2, lambda c: yT_s[:, c, :])

        S.final_wait("sp")
        with nc.Block() as block:
            S.emit(block)
    return nc


def _fm(v):
    v = np.asarray(v, np.float32)
    lead = v.shape[:-1]
    return np.moveaxis(v.reshape(lead + (16, 128)), -1, 0)


def _tile_w(W, kc):
    K, N = W.shape
    t = W.reshape(kc, 128, N // 128, 128).transpose(2, 1, 0, 3)
    return np.ascontiguousarray(t).reshape(N // 128, 128, kc * 128)


_NC_CACHE = {}


def kernel(x_prompt, x_sample, c_prompt, c_sample, state_conv, state_rnn, cache_k, cache_v,
           ada_w, ada_b, g_mix, g_ffn, rg_w_in, rg_w_gate, rg_conv_w, rg_conv_b, rg_w_a, rg_b_a,
           rg_w_i, rg_b_i, rg_lambda, rg_w_out, g_kv, w_k, w_v, w_q, w_o, rel_bias,
           ffn_w1, ffn_w3, ffn_w2, g_final):
    f32 = np.float32
    A = lambda a: np.asarray(a, f32)
    NCORE = 8
    vec_parts = {
        "g_mix": _fm(A(g_mix)), "g_ffn": _fm(A(g_ffn)), "conv_w": _fm(A(rg_conv_w)), "conv_b": _fm(A(rg_conv_b)),
        "b_a": _fm(A(rg_b_a)), "b_i": _fm(A(rg_b_i)), "lam": _fm(A(rg_lambda)), "g_kv": _fm(A(g_kv)),
        "g_final": _fm(A(g_final)), "ada_b": _fm(A(ada_b).reshape(4, 6, 2048)),
    }
    vecs = np.concatenate([vec_parts[n].reshape(128, -1) for n, _ in VEC_SPEC], axis=1)
    assert vecs.shape == (128, NV)
    vecs = np.ascontiguousarray(vecs, f32)
    w_ada_t = np.concatenate([_tile_w(A(ada_w)[L], 16) for L in range(4)], axis=0)
    w_in_t = np.concatenate([_tile_w(A(rg_w_in)[L], 16) for L in range(2)], axis=0)
    w_gate_t = np.concatenate([_tile_w(A(rg_w_gate)[L], 16) for L in range(2)], axis=0)
    w_out_t = np.concatenate([_tile_w(A(rg_w_out)[L], 16) for L in range(2)], axis=0)
    wa, wi = A(rg_w_a), A(rg_w_i)
    w_ai_t = np.zeros((16, 128, 2, 2, 256), f32)
    for L in range(2):
        for b in range(8):
            w_ai_t[L * 8 + b, :, 0] = wa[L, b].reshape(2, 128, 256).transpose(1, 0, 2)
            w_ai_t[L * 8 + b, :, 1] = wi[L, b].reshape(2, 128, 256).transpose(1, 0, 2)
    w_ai_t = w_ai_t.reshape(16, 128, 1024)
    w_k_t = _tile_w(A(w_k), 16)
    w_v_t = np.ascontiguousarray(A(w_v).reshape(4, 4, 128, 4, 512).transpose(3, 0, 2, 1, 4)).reshape(16, 128, 2048)
    w_q_t = np.concatenate([_tile_w(A(w_q)[L], 16) for L in range(2)], axis=0)
    w_o_t = np.concatenate([_tile_w(A(w_o)[L], 16) for L in range(2)], axis=0)
    w_1_t = np.concatenate([_tile_w(A(ffn_w1)[L], 16) for L in range(4)], axis=0)
    w_3_t = np.concatenate([_tile_w(A(ffn_w3)[L], 16) for L in range(4)], axis=0)
    w2 = A(ffn_w2)
    w_2_t = np.ascontiguousarray(w2.reshape(4, 4, 11, 128, 16, 128).transpose(0, 1, 4, 3, 2, 5)).reshape(256, 128, 1408)
    rb = A(rel_bias)
    kk = np.arange(128)[:, None]
    qq = np.arange(128)[None, :]
    i0 = qq - kk + 128
    i1 = np.minimum(qq - kk + 256, 256)
    dtab = np.zeros((2, 128, 16, 2, 128), f32)
    for bl in range(2):
        dtab[bl, :, :, 1, :] = rb[bl][i0].transpose(0, 2, 1)
        dtab[bl, :, :, 0, :] = rb[bl][i1].transpose(0, 2, 1)
    dtab = dtab.reshape(2, 128, 4096)
    cfar = np.ascontiguousarray(np.broadcast_to(rb[:, 256, :].reshape(1, 32), (128, 32)), f32)

    xp, xsm = A(x_prompt), A(x_sample)
    cp, csm = A(c_prompt), A(c_sample)
    sc, sr = A(state_conv), A(state_rnn)
    ck, cv = A(cache_k), A(cache_v)
    shared = dict(vecs=vecs, dtab=dtab, cfar=cfar, w_ada=w_ada_t, w_in=w_in_t, w_gate=w_gate_t, w_ai=w_ai_t,
                  w_out=w_out_t, w_k=w_k_t, w_v=w_v_t, w_q=w_q_t, w_o=w_o_t, w_1=w_1_t, w_3=w_3_t, w_2=w_2_t)
    in_maps = []
    for core in range(NCORE):
        m = dict(shared)
        xt = xp[core * 2:(core + 1) * 2].reshape(8, 512, 16, 128).transpose(0, 3, 2, 1)
        m["xT_p"] = np.ascontiguousarray(xt).reshape(8, 128, 8192)
        xs_ = xsm[core * 4:(core + 1) * 4].reshape(256, 16, 128).transpose(2, 1, 0)
        m["xT_s"] = np.ascontiguousarray(xs_)
        c6 = np.concatenate([cp[core * 2:(core + 1) * 2], csm[core * 4:(core + 1) * 4]], axis=0)
        m["cT"] = np.ascontiguousarray(c6.reshape(6, 16, 128).transpose(2, 1, 0)).reshape(128, 96)
        sc4 = sc[:, core * 4:(core + 1) * 4]
        m["conv_in"] = np.ascontiguousarray(sc4.reshape(2, 4, 3, 16, 128).transpose(4, 0, 3, 1, 2)).reshape(128, 384)
        sr4 = sr[:, core * 4:(core + 1) * 4]
        m["rnn_in"] = np.ascontiguousarray(sr4.reshape(2, 4, 16, 128).transpose(3, 0, 2, 1)).reshape(128, 128)
        ck4 = ck[core * 4:(core + 1) * 4]
        m["kTc"] = np.ascontiguousarray(ck4.transpose(0, 3, 2, 1)).reshape(4, 128, 8192)
        cv4 = cv[core * 4:(core + 1) * 4].reshape(4, 4, 128, 2048)
        m["vc"] = np.ascontiguousarray(cv4.transpose(0, 2, 1, 3)).reshape(4, 128, 8192)
        in_maps.append(m)

    if "nc" not in _NC_CACHE:
        _NC_CACHE["nc"] = build_nc()
    nc = _NC_CACHE["nc"]
    res = run_bass_kernel_spmd(nc, in_maps, core_ids=list(range(NCORE)))
    R = res.results

    y_prompt = np.zeros((16, 2048, 2048), f32)
    y_sample = np.zeros((32, 64, 2048), f32)
    conv_prompt = np.zeros((2, 16, 3, 2048), f32)
    rnn_prompt = np.zeros((2, 16, 2048), f32)
    k_prompt = np.zeros((16, 512, 16, 128), f32)
    v_prompt = np.zeros((16, 512, 16, 128), f32)
    conv_sample = np.zeros((2, 32, 3, 2048), f32)
    rnn_sample = np.zeros((2, 32, 2048), f32)
    k_sample = np.zeros((32, 64, 16, 128), f32)
    v_sample = np.zeros((32, 64, 16, 128), f32)
    for core in range(NCORE):
        r = R[core]
        yp = np.asarray(r["yT_p"]).reshape(2, 4, 128, 16, 512)
        y_prompt[core * 2:(core + 1) * 2] = yp.transpose(0, 1, 4, 3, 2).reshape(2, 2048, 2048)
        ys = np.asarray(r["yT_s"]).reshape(128, 16, 4, 64)
        y_sample[core * 4:(core + 1) * 4] = ys.transpose(2, 3, 1, 0).reshape(4, 64, 2048)
        cpo = np.asarray(r["conv_p"]).reshape(2, 128, 2, 16, 3)
        conv_prompt[:, core * 2:(core + 1) * 2] = cpo.transpose(2, 0, 4, 3, 1).reshape(2, 2, 3, 2048)
        rpo = np.asarray(r["rnn_p"]).reshape(2, 128, 2, 16)
        rnn_prompt[:, core * 2:(core + 1) * 2] = rpo.transpose(2, 0, 3, 1).reshape(2, 2, 2048)
        cso = np.asarray(r["conv_s"]).reshape(128, 2, 16, 4, 3)
        conv_sample[:, core * 4:(core + 1) * 4] = cso.transpose(1, 3, 4, 2, 0).reshape(2, 4, 3, 2048)
        rso = np.asarray(r["rnn_s"]).reshape(128, 2, 16, 4)
        rnn_sample[:, core * 4:(core + 1) * 4] = rso.transpose(1, 3, 2, 0).reshape(2, 4, 2048)
        kp = np.asarray(r["kT_p"]).reshape(2, 128, 16, 512)
        k_prompt[core * 2:(core + 1) * 2] = kp.transpose(0, 3, 2, 1)
        v_prompt[core * 2:(core + 1) * 2] = np.asarray(r["v_p"]).reshape(2, 512, 16, 128)
        ks = np.asarray(r["kT_s"]).reshape(128, 16, 4, 64)
        k_sample[core * 4:(core + 1) * 4] = ks.transpose(2, 3, 1, 0)
        v_sample[core * 4:(core + 1) * 4] = np.asarray(r["v_s"]).reshape(4, 64, 16, 128)
    return (y_prompt, y_sample, conv_prompt, rnn_prompt, k_prompt, v_prompt,
            conv_sample, rnn_sample, k_sample, v_sample)
```
